# Optimizing a Trainium2 kernel written in Bass

```python
import jax, jax.numpy as jnp
from jax import lax
import numpy as np

D_MODEL = 1024
BATCH = 4
SEQ = 4096
DEPTH = 1

HEAD_DIM = 64
N_ATTN_HEADS = 16
ATTN_WIDTH = N_ATTN_HEADS * HEAD_DIM
ROPE_DIM = HEAD_DIM // 4
ROPE_THETA = 500000.0
DILATED_PATTERNS = ((128, 1), (512, 4), (2048, 16))

D_INNER = 1024
SSM_HEAD_DIM = 64
N_SSM_HEADS = D_INNER // SSM_HEAD_DIM
N_SSM_GROUPS = 4
HEADS_PER_GROUP = N_SSM_HEADS // N_SSM_GROUPS
D_STATE = 128
SSM_CONV = 3
CHUNK = 128
XBC_WIDTH = D_INNER + 2 * N_SSM_GROUPS * D_STATE

MIX_WIDTH = ATTN_WIDTH + D_INNER
IN_WIDTH = 3 * ATTN_WIDTH + D_INNER + XBC_WIDTH + 2 * N_SSM_HEADS

D_FF = 2816
FFN_CONV = 3
EPS = 1e-6

kernel_name = "hymba_dilated_ssd_convffn_encoder"


def rmsnorm(x, w):
    xf = x.astype(jnp.float32)
    y = xf * lax.rsqrt(jnp.mean(xf * xf, axis=-1, keepdims=True) + EPS)
    return (y * w.astype(jnp.float32)).astype(x.dtype)


def dwconv_centered(x, w, b):
    k = w.shape[1]
    rhs = w.T[:, None, :].astype(x.dtype)
    y = lax.conv_general_dilated(x, rhs, window_strides=(1,), padding=[(k // 2, k // 2)],
                                 dimension_numbers=("NWC", "WIO", "NWC"),
                                 feature_group_count=x.shape[-1])
    return y + b.astype(x.dtype)


def partial_rope(t, pos):
    half = ROPE_DIM // 2
    inv_freq = jnp.power(ROPE_THETA, -jnp.arange(half, dtype=jnp.float32) * 2.0 / ROPE_DIM)
    ang = pos[:, None] * inv_freq[None, :]
    cos = jnp.cos(ang)[None, :, None, :]
    sin = jnp.sin(ang)[None, :, None, :]
    tf = t.astype(jnp.float32)
    x1 = tf[..., :half]
    x2 = tf[..., half:ROPE_DIM]
    out = jnp.concatenate([x1 * cos - x2 * sin, x2 * cos + x1 * sin, tf[..., ROPE_DIM:]], axis=-1)
    return out.astype(t.dtype)


def band_attention(q, k, v, half):
    n, length, h, dh = q.shape
    blk = half
    nb = -(-length // blk)
    lp = nb * blk
    qb = jnp.pad(q, [(0, 0), (0, lp - length), (0, 0), (0, 0)]).reshape(n, nb, blk, h, dh)
    pad_kv = [(0, 0), (blk, lp - length + blk), (0, 0), (0, 0)]
    kb = jnp.pad(k, pad_kv).reshape(n, nb + 2, blk, h, dh)
    vb = jnp.pad(v, pad_kv).reshape(n, nb + 2, blk, h, dh)
    kw = jnp.concatenate([kb[:, :-2], kb[:, 1:-1], kb[:, 2:]], axis=2)
    vw = jnp.concatenate([vb[:, :-2], vb[:, 1:-1], vb[:, 2:]], axis=2)
    s = jnp.einsum("nbqhd,nbkhd->nbhqk", qb, kw, preferred_element_type=jnp.float32) * (dh ** -0.5)
    qpos = jnp.arange(nb)[:, None] * blk + jnp.arange(blk)[None, :]
    kpos = jnp.arange(nb)[:, None] * blk - blk + jnp.arange(3 * blk)[None, :]
    valid = ((jnp.abs(kpos[:, None, :] - qpos[:, :, None]) <= half)
             & (kpos >= 0)[:, None, :] & (kpos < length)[:, None, :])
    s = jnp.where(valid[None, :, None], s, -jnp.inf)
    m = jnp.max(s, axis=-1, keepdims=True)
    p = jnp.exp(s - m)
    den = jnp.sum(p, axis=-1)
    o = jnp.einsum("nbhqk,nbkhd->nbqhd", p, vw.astype(jnp.float32))
    o = o / jnp.transpose(den, (0, 1, 3, 2))[..., None]
    lse = jnp.transpose(m[..., 0] + jnp.log(den), (0, 1, 3, 2))
    o = o.reshape(n, lp, h, dh)[:, :length]
    lse = lse.reshape(n, lp, h)[:, :length]
    return o, lse


def dilated_attention(q, k, v):
    b, s, h, dh = q.shape
    outs, lses = [], []
    for window, dil in DILATED_PATTERNS:
        half = (window // 2) // dil
        length = s // dil

        def gather(t):
            t = t.reshape(b, length, dil, h, dh).transpose(0, 2, 1, 3, 4)
            return t.reshape(b * dil, length, h, dh)

        o, lse = band_attention(gather(q), gather(k), gather(v), half)
        o = o.reshape(b, dil, length, h, dh).transpose(0, 2, 1, 3, 4).reshape(b, s, h, dh)
        lse = lse.reshape(b, dil, length, h).transpose(0, 2, 1, 3).reshape(b, s, h)
        outs.append(o)
        lses.append(lse)
    wts = jax.nn.softmax(jnp.stack(lses, axis=0), axis=0)[..., None]
    out = jnp.sum(wts * jnp.stack(outs, axis=0), axis=0)
    return out.astype(q.dtype)


def segsum(a):
    t = a.shape[-1]
    a_rep = jnp.broadcast_to(a[..., :, None], a.shape + (t,))
    a_rep = jnp.where(jnp.tril(jnp.ones((t, t), dtype=bool), -1), a_rep, 0.0)
    cs = jnp.cumsum(a_rep, axis=-2)
    return jnp.where(jnp.tril(jnp.ones((t, t), dtype=bool)), cs, -jnp.inf)


def ssd_scan(x, dt, a_head, bm, cm):
    b, l, g, r, p = x.shape
    c = l // CHUNK
    xc = (x * dt[..., None]).reshape(b, c, CHUNK, g, r, p)
    a = (dt * a_head).reshape(b, c, CHUNK, g, r).transpose(0, 3, 4, 1, 2)
    bc = bm.reshape(b, c, CHUNK, g, -1)
    cc = cm.reshape(b, c, CHUNK, g, -1)
    a_cs = jnp.cumsum(a, axis=-1)
    lmat = jnp.exp(segsum(a))
    cb = jnp.einsum("bclgn,bcsgn->bcgls", cc, bc)
    y_diag = jnp.einsum("bcgls,bgrcls,bcsgrp->bclgrp", cb, lmat, xc)
    decay_states = jnp.exp(a_cs[..., -1:] - a_cs)
    states = jnp.einsum("bclgn,bgrcl,bclgrp->bcgrpn", bc, decay_states, xc)
    states = jnp.concatenate([jnp.zeros_like(states[:, :1]), states], axis=1)
    chunk_tot = jnp.pad(a_cs[..., -1], [(0, 0), (0, 0), (0, 0), (1, 0)])
    decay_chunk = jnp.exp(segsum(chunk_tot))
    states = jnp.einsum("bgrzc,bcgrpn->bzgrpn", decay_chunk, states)[:, :-1]
    y_off = jnp.einsum("bclgn,bcgrpn,bgrcl->bclgrp", cc, states, jnp.exp(a_cs))
    return (y_diag + y_off).reshape(b, l, g, r, p)


def ssm_mixer(z, xbc, dt_f_raw, dt_b_raw, conv_w, conv_b, a_log_f, a_log_b,
              dt_bias_f, dt_bias_b, d_skip, norm_w):
    b, l, _ = z.shape
    xbc = jax.nn.silu(dwconv_centered(xbc, conv_w, conv_b))
    gn = N_SSM_GROUPS * D_STATE
    xs = xbc[..., :D_INNER].astype(jnp.float32).reshape(b, l, N_SSM_GROUPS, HEADS_PER_GROUP, SSM_HEAD_DIM)
    bm = xbc[..., D_INNER:D_INNER + gn].astype(jnp.float32).reshape(b, l, N_SSM_GROUPS, D_STATE)
    cm = xbc[..., D_INNER + gn:].astype(jnp.float32).reshape(b, l, N_SSM_GROUPS, D_STATE)

    def direction(xs_, bm_, cm_, dt_raw, a_log, dt_bias):
        dt = jax.nn.softplus(dt_raw.astype(jnp.float32) + dt_bias.astype(jnp.float32))
        dt = dt.reshape(b, l, N_SSM_GROUPS, HEADS_PER_GROUP)
        a_head = -jnp.exp(a_log.astype(jnp.float32)).reshape(N_SSM_GROUPS, HEADS_PER_GROUP)
        return ssd_scan(xs_, dt, a_head, bm_, cm_)

    flip = lambda t: jnp.flip(t, axis=1)
    y_f = direction(xs, bm, cm, dt_f_raw, a_log_f, dt_bias_f)
    y_b = flip(direction(flip(xs), flip(bm), flip(cm), flip(dt_b_raw), a_log_b, dt_bias_b))
    d = d_skip.astype(jnp.float32).reshape(N_SSM_GROUPS, HEADS_PER_GROUP)[..., None]
    y = (y_f + y_b + d * xs).reshape(b, l, D_INNER)
    gated = (y * jax.nn.silu(z.astype(jnp.float32))).reshape(b, l, N_SSM_GROUPS, D_INNER // N_SSM_GROUPS)
    gated = gated * lax.rsqrt(jnp.mean(gated * gated, axis=-1, keepdims=True) + EPS)
    return (gated.reshape(b, l, D_INNER) * norm_w.astype(jnp.float32)).astype(z.dtype)


def conv_gated_mlp(h, w_up, conv_w, conv_b, w_down):
    u = dwconv_centered(h @ w_up, conv_w, conv_b)
    gate = u[..., :D_FF]
    up = u[..., D_FF:]
    return (jax.nn.silu(gate) * up) @ w_down


def setup_inputs(seed: int = 0) -> dict:
    key = jax.random.key(seed)
    ks = jax.random.split(key, 20)
    f32 = jnp.float32

    def gain(k, shape):
        return 1.0 + 0.02 * jax.random.normal(k, shape, f32)

    def dt_bias(k):
        dt = jnp.exp(jax.random.uniform(k, (DEPTH, N_SSM_HEADS), f32, np.log(1e-3), np.log(1e-1)))
        return dt + jnp.log(-jnp.expm1(-dt))

    return {
        "x": jax.random.normal(ks[0], (BATCH, SEQ, D_MODEL), f32),
        "norm1_w": gain(ks[1], (DEPTH, D_MODEL)),
        "w_in": jax.random.normal(ks[2], (DEPTH, D_MODEL, IN_WIDTH), f32) * D_MODEL ** -0.5,
        "ssm_conv_w": jax.random.normal(ks[3], (DEPTH, XBC_WIDTH, SSM_CONV), f32) * SSM_CONV ** -0.5,
        "ssm_conv_b": 0.02 * jax.random.normal(ks[4], (DEPTH, XBC_WIDTH), f32),
        "a_log_f": jnp.log(jax.random.uniform(ks[5], (DEPTH, N_SSM_HEADS), f32, 1.0, 16.0)),
        "a_log_b": jnp.log(jax.random.uniform(ks[6], (DEPTH, N_SSM_HEADS), f32, 1.0, 16.0)),
        "dt_bias_f": dt_bias(ks[7]),
        "dt_bias_b": dt_bias(ks[8]),
        "d_skip": gain(ks[9], (DEPTH, N_SSM_HEADS)),
        "ssm_norm_w": gain(ks[10], (DEPTH, D_INNER)),
        "w_out": jax.random.normal(ks[11], (DEPTH, MIX_WIDTH, D_MODEL), f32) * MIX_WIDTH ** -0.5,
        "norm2_w": gain(ks[12], (DEPTH, D_MODEL)),
        "w_up": jax.random.normal(ks[13], (DEPTH, D_MODEL, 2 * D_FF), f32) * D_MODEL ** -0.5,
        "ffn_conv_w": jax.random.normal(ks[14], (DEPTH, 2 * D_FF, FFN_CONV), f32) * FFN_CONV ** -0.5,
        "ffn_conv_b": 0.02 * jax.random.normal(ks[15], (DEPTH, 2 * D_FF), f32),
        "w_down": jax.random.normal(ks[16], (DEPTH, D_FF, D_MODEL), f32) * D_FF ** -0.5,
        "final_norm_w": gain(ks[17], (D_MODEL,)),
    }


def reference(x, norm1_w, w_in, ssm_conv_w, ssm_conv_b, a_log_f, a_log_b, dt_bias_f, dt_bias_b,
              d_skip, ssm_norm_w, w_out, norm2_w, w_up, ffn_conv_w, ffn_conv_b, w_down, final_norm_w):
    b, s, _ = x.shape
    pos = jnp.arange(s, dtype=jnp.float32)
    sizes = [ATTN_WIDTH, ATTN_WIDTH, ATTN_WIDTH, D_INNER, XBC_WIDTH, N_SSM_HEADS, N_SSM_HEADS]
    splits = [int(v) for v in np.cumsum(sizes)[:-1]]
    for layer in range(DEPTH):
        h = rmsnorm(x, norm1_w[layer])
        proj = h @ w_in[layer]
        q, k, v, z, xbc, dt_f, dt_b = jnp.split(proj, splits, axis=-1)
        q = partial_rope(q.reshape(b, s, N_ATTN_HEADS, HEAD_DIM), pos)
        k = partial_rope(k.reshape(b, s, N_ATTN_HEADS, HEAD_DIM), pos)
        v = v.reshape(b, s, N_ATTN_HEADS, HEAD_DIM)
        attn = dilated_attention(q, k, v).reshape(b, s, ATTN_WIDTH)
        ssm = ssm_mixer(z, xbc, dt_f, dt_b, ssm_conv_w[layer], ssm_conv_b[layer], a_log_f[layer],
                        a_log_b[layer], dt_bias_f[layer], dt_bias_b[layer], d_skip[layer], ssm_norm_w[layer])
        x = x + jnp.concatenate([attn, ssm], axis=-1) @ w_out[layer]
        h = rmsnorm(x, norm2_w[layer])
        x = x + conv_gated_mlp(h, w_up[layer], ffn_conv_w[layer], ffn_conv_b[layer], w_down[layer])
    return rmsnorm(x, final_norm_w)
```

```python
import numpy as np
import ml_dtypes
from contextlib import ExitStack
import concourse.bass as bass
import concourse.mybir as mybir
from concourse.bass_utils import run_bass_kernel_spmd

F32 = mybir.dt.float32
BF16 = mybir.dt.bfloat16
AF = mybir.ActivationFunctionType
ALU = mybir.AluOpType

EPS = 1e-6
NTOK = 4096
OWN = 2048
NQ = 2049
NKV = 3073
NCH = 17
MIXW = 2176


class Tile:
    __slots__ = ("ap", "name", "w", "r", "dsem", "dcnt", "bap")

    def __init__(self, ap, name):
        self.ap = ap
        self.name = name
        self.w = None
        self.r = {}
        self.dsem = None
        self.dcnt = 0


class Ctx:
    ENGS = ("tensor", "vector", "scalar", "gpsimd", "sync")

    def __init__(self, nc, stack):
        self.nc = nc
        self.stack = stack
        self.lists = {e: [] for e in self.ENGS}
        self.sem = {}
        self.cnt = {e: 0 for e in self.ENGS}
        self.seen = {e: {} for e in self.ENGS}
        for e in self.ENGS:
            self.sem[e] = stack.enter_context(nc.semaphore("s_" + e))
        self.nsem = 0
        self.arena = None
        self.top = 0
        self.cap = 0
        self.uid = 0
        self.dma_last = {}
        self.marks = []
        self.abs = {e: [] for e in self.ENGS}

    def mark(self, name):
        self.marks.append((name, dict(self.cnt)))

    def barrier(self):
        evs = [(e, self.sem[e], self.cnt[e]) for e in self.ENGS if self.cnt[e] > 0]
        evs += list(self.dma_last.values())
        for e in self.ENGS:
            for ev in evs:
                if ev[0] != e:
                    self.wait_event(e, ev)

    def init_arena(self, ncols):
        self.arena = self.stack.enter_context(self.nc.sbuf_tensor("arena", [128, ncols], F32))
        self.cap = ncols
        self.top = 0

    def alloc(self, name, cols, dtype=F32):
        n32 = cols if dtype == F32 else (cols + 1) // 2
        assert self.top + n32 <= self.cap, f"arena overflow at {name}: {self.top}+{n32}>{self.cap}"
        ap = self.arena[:, self.top:self.top + n32]
        self.top += n32
        if dtype != F32:
            ap = ap.bitcast(dtype)[:, 0:cols]
        self.uid += 1
        return Tile(ap, f"{name}_{self.uid}")

    def psum(self, name, cols=512, dtype=F32):
        t = self.stack.enter_context(self.nc.psum_tensor(name, [128, cols], dtype))
        return Tile(t[:, :], name)

    def newsem(self, name):
        self.nsem += 1
        return self.stack.enter_context(self.nc.semaphore(f"d{self.nsem}"))

    def _waits(self, eng, reads, writes, skip_own):
        need = {}

        def add(ev):
            if ev is None:
                return
            k, s, v = ev
            if skip_own and k == eng:
                return
            if need.get(k, (None, 0))[1] < v:
                need[k] = (s, v)
        for t in reads:
            add(t.w)
        for t in writes:
            add(t.w)
            for ev in t.r.values():
                add(ev)
        for k, (s, v) in need.items():
            if self.seen[eng].get(k, 0) < v:
                self.seen[eng][k] = v
                self.lists[eng].append(lambda e, s=s, v=v: e.wait_ge(s, v))
                self.abs[eng].append(('w', k, v))

    def op(self, eng, fn, reads=(), writes=(), signal=True):
        self._waits(eng, reads, writes, skip_own=(eng == "tensor"))
        sem = self.sem[eng]
        if signal:
            self.cnt[eng] += 1
            self.lists[eng].append(lambda e, fn=fn, sem=sem: fn(e).then_inc(sem, 1))
            self.abs[eng].append(('i', eng, 1))
            ev = (eng, sem, self.cnt[eng])
        else:
            self.lists[eng].append(lambda e, fn=fn: fn(e))
            ev = (eng, sem, self.cnt[eng] + 1)
        for t in reads:
            if t.r.get(eng, (None, None, 0))[2] < ev[2]:
                t.r[eng] = ev
        for t in writes:
            t.w = ev
            t.r = {}
        return ev

    def dma(self, out_ap, in_ap, reads=(), writes=(), q="sync", sem_tile=None):
        st = sem_tile or (writes[0] if writes else reads[0])
        if st.dsem is None:
            st.dsem = self.newsem(st.name)
        key = "dma:" + st.name
        need_w = []
        for t in writes:
            if t.w is not None and t.w[0] == key and not t.r:
                continue
            need_w.append(t)
        self._waits(q, reads, need_w, skip_own=False)
        st.dcnt += 16
        sem = st.dsem
        ev = (key, sem, st.dcnt)
        self.lists[q].append(lambda e, o=out_ap, i=in_ap, sem=sem: e.dma_start(out=o, in_=i).then_inc(sem, 16))
        self.abs[q].append(('i', key, 16))
        self.dma_last[key] = ev
        for t in reads:
            t.r[key] = ev
        for t in writes:
            t.w = ev
            t.r = {}
        return ev

    def wait_event(self, eng, ev):
        k, s, v = ev
        if self.seen[eng].get(k, 0) < v:
            self.seen[eng][k] = v
            self.lists[eng].append(lambda e, s=s, v=v: e.wait_ge(s, v))
            self.abs[eng].append(('w', k, v))

    def emit(self):
        with self.nc.Block() as block:
            @block.sync
            def _(e):
                for f in self.lists["sync"]:
                    f(e)

            @block.tensor
            def _(e):
                for f in self.lists["tensor"]:
                    f(e)

            @block.vector
            def _(e):
                for f in self.lists["vector"]:
                    f(e)

            @block.scalar
            def _(e):
                for f in self.lists["scalar"]:
                    f(e)

            @block.gpsimd
            def _(e):
                for f in self.lists["gpsimd"]:
                    f(e)


def build_nc(debug=False):
    nc = bass.Bass("TRN2", target_bir_lowering=False)

    def din(name, shape, dt=F32):
        return nc.dram_tensor(name, list(shape), dt, kind="ExternalInput").ap()

    x = din("x", [NTOK, 1024])
    w_in = din("w_in", [1024, 6176])
    w_out = din("w_out", [2048, 1024])
    w_up = din("w_up", [1024, 5632])
    w_down = din("w_down", [2816, 1024])
    d_nw1 = din("nw1", [128, 8])
    d_nw2 = din("nw2", [128, 8])
    d_nwf = din("nwf_bc", [128, 1024])
    d_cw = din("cw", [128, 48])
    d_cb = din("cb", [128, 16])
    d_fcw = din("fcw", [128, 132])
    d_fcb = din("fcb", [128, 44])
    d_alog = din("alog_bc", [128, 32])
    d_dtb = din("dtb_bc", [128, 32])
    d_dsk = din("dskip_bc", [128, 16])
    d_snw = din("ssm_nw_bc", [128, 1024])
    d_ident = din("ident", [128, 128])
    d_U = din("U", [128, 128])
    d_UsN = din("UsN", [128, 128])
    d_ones = din("ones", [128, 128])
    d_sel4 = din("sel4", [4, 512])
    d_maskb = din("maskb", [128, 256])
    d_amask = din("amask", [128, 17 * 128])
    d_cos = din("ropecos", [128, 3200])
    d_sin = din("ropesin", [128, 3200])
    d_hmask = din("hmask", [128, 2])
    out = nc.dram_tensor("out", [OWN, 1024], F32, kind="ExternalOutput").ap()
    mixT = nc.dram_tensor("mixT", [2048, MIXW], BF16, kind=("ExternalOutput" if debug else "Internal")).ap()

    with ExitStack() as st:
        cx = Ctx(nc, st)
        cx.init_arena(53200)
        op = cx.op

        def act(out_, in_, func, reads, writes, **kw):
            return op("scalar", lambda e: e.activation(out=out_, in_=in_, func=func, **kw), reads, writes)

        def tt(eng, out_, in0, in1, o, reads, writes):
            return op(eng, lambda e: e.tensor_tensor(out=out_, in0=in0, in1=in1, op=o), reads, writes)

        def ts(eng, out_, in0, s1, s2, o0, o1, reads, writes):
            if s2 is None:
                return op(eng, lambda e: e.tensor_scalar(out=out_, in0=in0, scalar1=s1, scalar2=None, op0=o0), reads, writes)
            return op(eng, lambda e: e.tensor_scalar(out=out_, in0=in0, scalar1=s1, scalar2=s2, op0=o0, op1=o1), reads, writes)

        def stt(eng, out_, in0, sc, in1, o0, o1, reads, writes):
            return op(eng, lambda e: e.scalar_tensor_tensor(out=out_, in0=in0, scalar=sc, in1=in1, op0=o0, op1=o1), reads, writes)

        def cp(eng, out_, in_, reads, writes):
            if eng == "scalar":
                return act(out_, in_, AF.Copy, reads, writes)
            return op(eng, lambda e: e.tensor_copy(out=out_, in_=in_), reads, writes)

        def mm(out_, lhsT, rhs, start, stop, reads, writes):
            return op("tensor", lambda e: e.matmul(out=out_, lhsT=lhsT, rhs=rhs, start=start, stop=stop), reads, writes, signal=stop)

        def tr(out_, in_, ident, reads, writes, signal=True):
            return op("tensor", lambda e: e.transpose(out=out_, in_=in_, identity=ident), reads, writes, signal=signal)

        def memset(eng, t, ap, val):
            return op(eng, lambda e: e.memset(ap, val), (), [t])

        def bc(ap2, n):
            return ap2.unsqueeze(2).to_broadcast([ap2.shape[0], ap2.shape[1], n])

        identf = cx.alloc("identf", 128)
        identb = cx.alloc("identb", 128, BF16)
        epst = cx.alloc("eps", 1)
        onet = cx.alloc("one", 1)
        junk = cx.alloc("junk", 1024, BF16)
        cx.dma(identf.ap, d_ident, writes=[identf])
        cp("vector", identb.ap, identf.ap, [identf], [identb])
        memset("vector", epst, epst.ap, EPS)
        memset("vector", onet, onet.ap, 1.0)
        PS = [cx.psum(f"ps{i}", 512, F32) for i in range(8)]
        for t_ in PS:
            t_.bap = t_.ap.bitcast(BF16)
        PT = [PS[6], PS[7]]
        persist_top = cx.top
        hT = cx.alloc("hT", 8 * NTOK, BF16)
        hT3 = hT.ap.rearrange("p (c t) -> p c t", c=8)
        mixtile = Tile(None, "mixtile")
        dscr = nc.dram_tensor("dscr", [16, MIXW], F32, kind="Internal").ap()

        def wload(dst3, src, nck, tile_, reads=()):
            for kc_ in range(nck):
                cx.dma(dst3[:, kc_, :], src[kc_ * 128:(kc_ + 1) * 128, :], reads=list(reads), writes=[tile_], q="gpsimd")

        def load_norm_transpose(xt_tile, xap, rows, nw_t, dst3, col0, xs_t, rs_t, ss_t, ptile, dve_scale=False):
            act(junk.ap[0:rows, :], xap[0:rows, :], AF.Square, [xt_tile], [junk, ss_t], accum_out=ss_t.ap[0:rows, :])
            act(rs_t.ap[0:rows, :], ss_t.ap[0:rows, :], AF.Ln, [ss_t, epst], [rs_t], scale=1.0 / 1024, bias=epst.ap[0:rows, :])
            act(rs_t.ap[0:rows, :], rs_t.ap[0:rows, :], AF.Exp, [rs_t], [rs_t], scale=-0.5)
            if dve_scale:
                ts("vector", xs_t.ap[0:rows, :], xap[0:rows, :], rs_t.ap[0:rows, :], None, ALU.mult, None, [xt_tile, rs_t], [xs_t])
            else:
                act(xs_t.ap[0:rows, :], xap[0:rows, :], AF.Copy, [xt_tile, rs_t], [xs_t], scale=rs_t.ap[0:rows, :])
            for kc in range(8):
                tr(ptile.bap[:, kc * 128:kc * 128 + rows], xs_t.ap[0:rows, kc * 128:(kc + 1) * 128], identb.ap[0:rows, 0:rows],
                   [xs_t, identb], [ptile], signal=(kc == 7))
            tt("vector", dst3[:, :, col0:col0 + rows], ptile.bap.rearrange("p (c t) -> p c t", c=8)[:, :, 0:rows],
               bc(nw_t.ap, rows), ALU.mult, [ptile, nw_t], [])

        nw1 = cx.alloc("nw1", 8)
        cx.dma(nw1.ap, d_nw1, writes=[nw1])
        p1_keep = cx.top
        xt = [cx.alloc(f"xt{i}", 1024) for i in range(2)]
        xs = [cx.alloc(f"xs{i}", 1024, BF16) for i in range(2)]
        ssq = [cx.alloc(f"ss{i}", 1) for i in range(2)]
        rsq = [cx.alloc(f"rs{i}", 1) for i in range(2)]
        for i in range(32):
            b = i % 2
            cx.dma(xt[b].ap, x[i * 128:(i + 1) * 128, :], writes=[xt[b]])
            load_norm_transpose(xt[b], xt[b].ap, 128, nw1, hT3, i * 128, xs[b], rsq[b], ssq[b], PT[b], dve_scale=True)
            hT.w = (("vector"), cx.sem["vector"], cx.cnt["vector"])
        p1_top = cx.top

        cx.mark('P1 end')
        cx.barrier()
        cx.top = p1_keep
        wdt = cx.alloc("wdt", 8 * 32, BF16)
        wload(wdt.ap.rearrange("p (c n) -> p c n", c=8), w_in[:, 6144:6176], 8, wdt)
        alog = cx.alloc("alog", 32)
        dtb = cx.alloc("dtb", 32)
        dsk = cx.alloc("dsk", 16)
        snw = cx.alloc("snw", 1024)
        Ut = cx.alloc("U", 128)
        UsN = cx.alloc("UsN", 128)
        onesf = cx.alloc("onesf", 128)
        sel4 = cx.alloc("sel4", 512)
        maskbf = cx.alloc("maskbf", 256)
        maskb = cx.alloc("maskb", 256, BF16)
        cwt = cx.alloc("cw", 48)
        cbt = cx.alloc("cb", 16)
        for t_, d_ in ((alog, d_alog), (dtb, d_dtb), (dsk, d_dsk), (snw, d_snw), (Ut, d_U), (UsN, d_UsN), (onesf, d_ones),
                       (maskbf, d_maskb), (cwt, d_cw), (cbt, d_cb)):
            cx.dma(t_.ap, d_, writes=[t_])
        cx.dma(sel4.ap[0:4, :], d_sel4, writes=[sel4])
        cp("vector", maskb.ap, maskbf.ap, [maskbf], [maskb])
        DT = cx.alloc("DT", 1024)
        AA = cx.alloc("AA", 1024)
        NEGV = cx.alloc("NEGV", 1024)
        EY = cx.alloc("EY", 1024)
        WW = cx.alloc("WW", 1024)
        DEC = cx.alloc("DEC", 1024)
        A3s = [cx.alloc(f"Asp{i}", 1024, BF16) for i in range(3)]
        Ub = cx.alloc("Ub", 128, BF16)
        UsNb = cx.alloc("UsNb", 128, BF16)
        onesb = cx.alloc("onesb", 128, BF16)
        sel4b = cx.alloc("sel4b", 512, BF16)
        vsp = [cx.alloc(f"vsp{i}", 256, BF16) for i in range(3)]
        vR = cx.alloc("vR", 256)
        wg = cx.alloc("wg", 8 * 512, BF16)
        wz = cx.alloc("wz", 8 * 256, BF16)

        def load_wg(g_):
            cols = [(4096 + g_ * 256, 256), (4096 + 1024 + g_ * 128, 128), (4096 + 1536 + g_ * 128, 128)]
            off = 0
            for (c0, wd_) in cols:
                wload(wg.ap.rearrange("p (c n) -> p c n", c=8)[:, :, off:off + wd_], w_in[:, c0:c0 + wd_], 8, wg)
                off += wd_

        def load_wz(g_):
            wload(wz.ap.rearrange("p (c n) -> p c n", c=8), w_in[:, 3072 + g_ * 256:3072 + (g_ + 1) * 256], 8, wz)
        load_wg(0)
        load_wz(0)
        pre_top = cx.top
        CS = cx.alloc("CS", 1024)
        TT = cx.alloc("TT", 1024)
        VV = cx.alloc("VV", 1024)

        def v3(t):
            return t.ap.rearrange("p (c j) -> p c j", j=32)
        for cg in range(2):
            pp = PS[cg]
            for ci in range(16):
                c = cg * 16 + ci
                for kc in range(8):
                    mm(pp.ap[:, ci * 32:(ci + 1) * 32], hT3[:, kc, c * 128:(c + 1) * 128], wdt.ap[:, kc * 32:(kc + 1) * 32],
                       kc == 0, kc == 7, [hT, wdt], [pp])
            tt("vector", DT.ap[:, cg * 512:(cg + 1) * 512].rearrange("p (c j) -> p c j", j=32), pp.ap.rearrange("p (c j) -> p c j", j=32),
               dtb.ap.unsqueeze(1).to_broadcast([128, 16, 32]), ALU.add, [pp, dtb], [DT])
        act(DT.ap, DT.ap, AF.Exp, [DT], [DT])
        ts("vector", CS.ap, DT.ap, 1.0, None, ALU.add, None, [DT], [CS])
        act(DT.ap, CS.ap, AF.Ln, [CS], [DT])
        for _it in range(2):
            act(TT.ap, DT.ap, AF.Exp, [DT], [TT], scale=-1.0)
            tt("vector", TT.ap, TT.ap, CS.ap, ALU.mult, [TT, CS], [TT])
            stt("vector", DT.ap, TT.ap, -1.0, DT.ap, ALU.add, ALU.add, [TT, DT], [DT])
        act(alog.ap, alog.ap, AF.Exp, [alog], [alog])
        ts("vector", alog.ap, alog.ap, -1.0, None, ALU.mult, None, [alog], [alog])
        tt("vector", v3(AA), v3(DT), alog.ap.unsqueeze(1).to_broadcast([128, 32, 32]), ALU.mult, [DT, alog], [AA])
        cp("vector", Ub.ap, Ut.ap, [Ut], [Ub])
        cp("vector", UsNb.ap, UsN.ap, [UsN], [UsNb])
        cp("vector", onesb.ap, onesf.ap, [onesf], [onesb])
        cp("vector", sel4b.ap[0:4, :], sel4.ap[0:4, :], [sel4], [sel4b])
        for cg in range(2):
            p1_, p2_ = PS[2 + cg], PS[4 + cg]
            for ci in range(16):
                c = cg * 16 + ci
                mm(p1_.ap[:, ci * 32:(ci + 1) * 32], Ut.ap, AA.ap[:, c * 32:(c + 1) * 32], True, True, [Ut, AA], [p1_])
                mm(p2_.ap[:, ci * 32:(ci + 1) * 32], onesf.ap, AA.ap[:, c * 32:(c + 1) * 32], True, True, [onesf, AA], [p2_])
            cp("vector", CS.ap[:, cg * 512:(cg + 1) * 512], p1_.ap, [p1_], [CS])
            cp("scalar", TT.ap[:, cg * 512:(cg + 1) * 512], p2_.ap, [p2_], [TT])
        cp("vector", v3(VV)[:, :, 0:16], v3(CS)[:, :, 0:16], [CS], [VV])
        tt("vector", v3(VV)[:, :, 16:32], v3(AA)[:, :, 16:32], v3(CS)[:, :, 16:32], ALU.subtract, [AA, CS], [VV])
        ts("vector", NEGV.ap, VV.ap, -1.0, None, ALU.mult, None, [VV], [NEGV])
        cp("vector", v3(EY)[:, :, 0:16], v3(VV)[:, :, 0:16], [VV], [EY])
        tt("vector", v3(EY)[:, :, 16:32], v3(TT)[:, :, 16:32], v3(VV)[:, :, 16:32], ALU.add, [TT, VV], [EY])
        act(EY.ap, EY.ap, AF.Exp, [EY], [EY])
        tt("vector", v3(WW)[:, :, 0:16], v3(TT)[:, :, 0:16], v3(VV)[:, :, 0:16], ALU.subtract, [TT, VV], [WW])
        cp("vector", v3(WW)[:, :, 16:32], v3(NEGV)[:, :, 16:32], [NEGV], [WW])
        act(WW.ap, WW.ap, AF.Exp, [WW], [WW])
        tt("vector", WW.ap, WW.ap, DT.ap, ALU.mult, [WW, DT], [WW])
        act(DEC.ap, TT.ap, AF.Exp, [TT], [DEC])

        cx.mark('P2 pre end')
        cx.barrier()
        cx.top = pre_top
        _raw0 = cx.alloc("raw0", 4098, BF16)
        rawb = [_raw0, _raw0]
        acttile = [cx.alloc(f"actt{i}", 512, BF16) for i in range(2)]
        xtok = cx.alloc("xtok", 32 * 256, BF16)
        btok = cx.alloc("btok", 32 * 128, BF16)
        BTt = cx.alloc("BT", 2560, BF16)
        CTt = cx.alloc("CT", 2560, BF16)
        ypart = cx.alloc("ypart", NCH * 256, BF16)
        Sloc = cx.alloc("Sloc", NCH * 256, BF16)
        diag = cx.alloc("diag", 12 * 128, BF16)
        Sst = cx.alloc("S", 256)
        Sbf = cx.alloc("Sbf", 256, BF16)
        Xw = [cx.alloc(f"Xw{i}", 256, BF16) for i in range(2)]
        vT = cx.alloc("vT", 256)
        vT2 = [vT, cx.alloc("vTb", 256)]
        SC = cx.alloc("SC", 512)
        vhl = [(cx.alloc(f"vh{i}", 256, BF16), cx.alloc(f"vl{i}", 256, BF16)) for i in range(2)]
        XV = [cx.alloc(f"XV{i}", 1024, BF16) for i in range(2)]
        Lexp = [cx.alloc(f"Lexp{i}", 128, BF16) for i in range(3)]
        MT = [cx.alloc(f"MT{i}", 128, BF16) for i in range(3)]
        t1s = [cx.alloc(f"t1_{i}", 256) for i in range(2)]
        t1 = t1s[0]
        t2 = cx.alloc("t2", 256)
        szt = cx.alloc("sz", 256)
        sst = cx.alloc("ssg", NCH)
        rstg = cx.alloc("rstg", NCH)
        ostg = [cx.alloc(f"ostg{i}", 256, BF16) for i in range(2)]
        ot = cx.alloc("ot", 256, BF16)
        memset("vector", _raw0, _raw0.ap, 0.0)
        xtok3 = xtok.ap.rearrange("p (c n) -> p c n", n=256)
        btok3 = btok.ap.rearrange("p (c n) -> p c n", n=128)
        EY3, WW3, DEC3, DT3 = v3(EY), v3(WW), v3(DEC), v3(DT)

        def hb(ap2):
            return bc(ap2, 64)

        def r3(ap):
            return ap.rearrange("p (j d) -> p j d", d=64)

        for g in range(4):
            if g > 0:
                load_wz(g)
            chunk_ids = [2 * g, 2 * g + 1, 8 + g, 12 + g]
            for jj in range(4):
                for k in range(3):
                    ts("vector", diag.ap[:, (jj * 3 + k) * 128:(jj * 3 + k + 1) * 128], identf.ap,
                       cwt.ap[:, chunk_ids[jj] * 3 + k:chunk_ids[jj] * 3 + k + 1], None, ALU.mult, None, [identf, cwt], [diag])
            wg3 = wg.ap.rearrange("p (c n) -> p c n", c=8)
            bcount = [0]
            pi = 0
            fifoB, fifoC = [], []
            bi = 0
            for jj in range(4):
                ntok = NTOK if jj < 3 else 2560
                raw = rawb[0]
                ngr = ntok // 512
                if jj == 3:
                    while fifoB:
                        fifoB.pop(0)()
                for tg in range(ngr):
                    pp = PS[pi % 3]
                    pi += 1
                    for kc in range(8):
                        mm(pp.ap, wg3[:, kc, jj * 128:(jj + 1) * 128], hT3[:, kc, tg * 512:(tg + 1) * 512], kc == 0, kc == 7, [wg, hT], [pp])
                    cp("vector", raw.ap[:, 1 + tg * 512:1 + (tg + 1) * 512], pp.ap, [pp], [raw])
                    if jj == 3 and tg == ngr - 1:
                        memset("vector", raw, raw.ap[:, 2561:2562], 0.0)

                    def stageB(jj=jj, tg=tg, raw=raw):
                        nonlocal_bi = stageB.counter[0]
                        stageB.counter[0] += 1
                        pc = PS[3 + (nonlocal_bi % 3)]
                        for k in range(3):
                            mm(pc.ap, diag.ap[:, (jj * 3 + k) * 128:(jj * 3 + k + 1) * 128], raw.ap[:, tg * 512 + k:tg * 512 + k + 512],
                               k == 0, k == 2, [diag, raw], [pc])
                        bias = cbt.ap[:, chunk_ids[jj]:chunk_ids[jj] + 1]
                        if jj == 3:
                            act(CTt.ap[:, tg * 512:(tg + 1) * 512], pc.ap, AF.Silu, [pc, cbt], [CTt], bias=bias)
                            return
                        if jj == 2 and tg < 5:
                            srct, src = BTt, BTt.ap[:, tg * 512:(tg + 1) * 512]
                        else:
                            srct = acttile[nonlocal_bi % 2]
                            src = srct.ap
                        act(src, pc.ap, AF.Silu, [pc, cbt], [srct], bias=bias)

                        def stageC(jj=jj, tg=tg, srct=srct, src=src, k_=nonlocal_bi):
                            ptile = PT[k_ % 2]
                            for sb in range(4):
                                tr(ptile.bap[:, sb * 128:(sb + 1) * 128], src[:, sb * 128:(sb + 1) * 128], identb.ap, [srct, identb], [ptile], signal=(sb == 3))
                            pv = ptile.bap[:, 0:512].rearrange("p (c n) -> p c n", n=128)
                            if jj < 2:
                                cp("vector", xtok3[:, tg * 4:(tg + 1) * 4, jj * 128:(jj + 1) * 128], pv, [ptile], [xtok])
                            else:
                                cp("vector", btok3[:, tg * 4:(tg + 1) * 4, :], pv, [ptile], [btok])
                        fifoC.append(stageC)
                        while len(fifoC) > 1:
                            fifoC.pop(0)()
                    stageB.counter = bcount
                    fifoB.append(stageB)
                    while len(fifoB) > 2:
                        fifoB.pop(0)()
            while fifoB:
                fifoB.pop(0)()
            while fifoC:
                fifoC.pop(0)()

            cx.mark(f'S1 end g{g}')
            if g < 3:
                load_wg(g + 1)
            memset("vector", Sst, Sst.ap, 0.0)
            memset("vector", Sbf, Sbf.ap, 0.0)
            fifo = []
            li = 0
            SC3 = SC.ap.rearrange("p (c v) -> p c v", v=16)
            for vi_, (src3, c0_) in enumerate(((DT3, g * 4), (DT3, 16 + g * 4), (WW3, g * 4), (WW3, 16 + g * 4))):
                cp("vector", SC3[:, :, vi_ * 4:(vi_ + 1) * 4], src3[:, :, c0_:c0_ + 4], [DT, WW], [SC])
            for c in range(31, -1, -1):
                own = c < NCH
                tok = c * 128
                b_ = c % 2
                hc = 16 + g * 4
                xv = XV[b_]
                xv4 = xv.ap.rearrange("p (v j d) -> p v j d", v=4, d=64)
                xw_ap = xv.ap[:, 768:1024]
                if own:
                    tt("vector", xv4, r3(xtok3[:, c, :]).unsqueeze(1).to_broadcast([128, 4, 4, 64]),
                       SC3[:, c, :].rearrange("p (v j) -> p v j", v=4).unsqueeze(3).to_broadcast([128, 4, 4, 64]), ALU.mult, [xtok, SC], [xv])
                    xd_ap = (xv.ap[:, 0:256], xv.ap[:, 256:512])
                    xwf_ap = xv.ap[:, 512:768]
                else:
                    tt("vector", r3(xw_ap), r3(xtok3[:, c, :]), hb(WW3[:, c, hc:hc + 4]), ALU.mult, [xtok, WW], [xv])
                if own:
                    pGV = PS[3 + b_]
                    pY = PS[5]
                    pOt = PS[6]
                    mm(pGV.ap[:, 0:128], BTt.ap[:, tok:tok + 128], CTt.ap[:, tok:tok + 128], True, True, [BTt, CTt], [pGV])
                    vTc = vT2[b_]
                    mm(pGV.ap[0:4, 128:256], AA.ap[:, c * 32 + g * 4:c * 32 + g * 4 + 4], Ut.ap, True, True, [AA, Ut], [pGV])
                    mm(pGV.ap[0:4, 256:384], AA.ap[:, c * 32 + 16 + g * 4:c * 32 + 16 + g * 4 + 4], UsN.ap, True, True, [AA, UsN], [pGV])
                    cp("vector", vTc.ap[0:4, :], pGV.ap[0:4, 128:384], [pGV], [vTc])
                    vh, vl = vhl[b_]
                    cp("scalar", vh.ap[0:4, :], pGV.ap[0:4, 128:384], [pGV], [vh])
                    tt("vector", vl.ap[0:4, :], vTc.ap[0:4, :], vh.ap[0:4, :], ALU.subtract, [vTc, vh], [vl])
                    mm(pOt.ap[:, 0:256], CTt.ap[:, tok:tok + 128], Sbf.ap, True, True, [CTt, Sbf], [pOt])
                    t1c = t1s[b_]
                    tt("vector", r3(t1c.ap), r3(pOt.ap[:, 0:256]), hb(EY3[:, c, hc:hc + 4]), ALU.mult, [pOt, EY], [t1c])
                    mm(pOt.ap[:, 256:512], btok3[:, c, :], xwf_ap, True, True, [btok, xv], [pOt])
                    cp("scalar", Sloc.ap[:, c * 256:(c + 1) * 256], pOt.ap[:, 256:512], [pOt], [Sloc])
                if c > 0:
                    pS = PS[7]
                    mm(pS.ap[:, 0:256], btok3[:, c, :], xw_ap, True, True, [btok, xv], [pS])
                    tt("vector", r3(Sst.ap), r3(Sst.ap), hb(DEC3[:, c, hc:hc + 4]), ALU.mult, [Sst, DEC], [Sst])
                    tt("vector", Sst.ap, Sst.ap, pS.ap[:, 0:256], ALU.add, [Sst, pS], [Sst])
                    cp("scalar", Sbf.ap, Sst.ap, [Sst], [Sbf])
                if own:
                    for j in range(4):
                        for d_ in range(2):
                            pL = PS[li % 3]
                            lx, mt = Lexp[li % 3], MT[li % 3]
                            li += 1
                            mm(pL.ap[:, 0:128], sel4b.ap[0:4, j * 128:(j + 1) * 128], vh.ap[0:4, d_ * 128:(d_ + 1) * 128], True, False, [sel4b, vh], [pL])
                            mm(pL.ap[:, 0:128], sel4b.ap[0:4, j * 128:(j + 1) * 128], vl.ap[0:4, d_ * 128:(d_ + 1) * 128], False, False, [sel4b, vl], [pL])
                            mm(pL.ap[:, 0:128], identb.ap, maskb.ap[:, d_ * 128:(d_ + 1) * 128], False, True, [identb, maskb], [pL])
                            col = c * 32 + d_ * 16 + g * 4 + j
                            act(lx.ap, pL.ap[:, 0:128], AF.Exp, [pL, NEGV], [lx], bias=NEGV.ap[:, col:col + 1])
                            tt("vector", mt.ap, lx.ap, pGV.ap[:, 0:128], ALU.mult, [lx, pGV], [mt])

                            def ymm(j=j, d_=d_, mt=mt, xd_ap=xd_ap, xv=xv, pY=pY):
                                op("tensor", lambda e: e.matmul(out=pY.ap[:, j * 64:(j + 1) * 64], lhsT=mt.ap, rhs=xd_ap[d_][:, j * 64:(j + 1) * 64],
                                                                start=(d_ == 0), stop=(d_ == 1)),
                                   [mt, xv], [pY], signal=(d_ == 1 and j == 3))
                            fifo.append(ymm)
                            while len(fifo) > 2:
                                fifo.pop(0)()

                    def tail(c=c, pY=pY, t1c=t1c):
                        tt("vector", t2.ap, t1c.ap, pY.ap[:, 0:256], ALU.add, [t1c, pY], [t2])
                        stt("vector", r3(ypart.ap[:, c * 256:(c + 1) * 256]), r3(xtok3[:, c, :]), 1.0, hb(dsk.ap[:, g * 4:g * 4 + 4]), ALU.mult, ALU.mult,
                            [xtok, dsk], [ypart])
                        tt("vector", ypart.ap[:, c * 256:(c + 1) * 256], ypart.ap[:, c * 256:(c + 1) * 256], t2.ap, ALU.add, [ypart, t2], [ypart])
                    fifo.append(tail)
            while fifo:
                fifo.pop(0)()

            cx.mark(f'passA end g{g}')
            memset("vector", Sst, Sst.ap, 0.0)
            memset("vector", Sbf, Sbf.ap, 0.0)
            Sbfs = [Sbf, Xw[0]]
            wz3 = wz.ap.rearrange("p (c n) -> p c n", c=8)
            for c in range(NCH):
                tok = c * 128
                sb_cur = Sbfs[c % 2]
                pO = PS[c % 2]
                mm(pO.ap[:, 0:256], CTt.ap[:, tok:tok + 128], sb_cur.ap, True, True, [CTt, sb_cur], [pO])
                if c < NCH - 1:
                    tt("vector", r3(Sst.ap), r3(Sst.ap), hb(DEC3[:, c, g * 4:g * 4 + 4]), ALU.mult, [Sst, DEC], [Sst])
                    tt("vector", Sst.ap, Sst.ap, Sloc.ap[:, c * 256:(c + 1) * 256], ALU.add, [Sst, Sloc], [Sst])
                    cp("scalar", Sbfs[(c + 1) % 2].ap, Sst.ap, [Sst], [Sbfs[(c + 1) % 2]])
                t1c = t1s[c % 2]
                tt("vector", r3(t1c.ap), r3(pO.ap[:, 0:256]), hb(EY3[:, c, g * 4:g * 4 + 4]), ALU.mult, [pO, EY], [t1c])
                tt("vector", t1c.ap, t1c.ap, ypart.ap[:, c * 256:(c + 1) * 256], ALU.add, [t1c, ypart], [t1c])
                pZ = PS[2 + c % 2]
                for kc in range(8):
                    mm(pZ.ap[:, 0:256], hT3[:, kc, tok:tok + 128], wz3[:, kc, :], kc == 0, kc == 7, [hT, wz], [pZ])
                act(szt.ap, pZ.ap[:, 0:256], AF.Silu, [pZ], [szt])
                tt("vector", t2.ap, t1c.ap, szt.ap, ALU.mult, [t1c, szt], [t2])
                act(junk.ap[:, 0:256], t2.ap, AF.Square, [t2], [junk, sst], accum_out=sst.ap[:, c:c + 1])
                cp("vector", ypart.ap[:, c * 256:(c + 1) * 256], t2.ap, [t2], [ypart])
            act(rstg.ap, sst.ap, AF.Ln, [sst, epst], [rstg], scale=1.0 / 256, bias=epst.ap)
            act(rstg.ap, rstg.ap, AF.Exp, [rstg], [rstg], scale=-0.5)
            for c in range(NCH):
                stt("vector", ot.ap, ypart.ap[:, c * 256:(c + 1) * 256], rstg.ap[:, c:c + 1], snw.ap[:, g * 256:(g + 1) * 256], ALU.mult, ALU.mult,
                    [ypart, rstg, snw], [ot])
                ptile = PT[c % 2]
                for jj in range(2):
                    tr(ptile.bap[:, jj * 128:(jj + 1) * 128], ot.ap[:, jj * 128:(jj + 1) * 128], identb.ap, [ot, identb], [ptile], signal=(jj == 1))
                og = ostg[c % 2]
                cp("vector", og.ap.rearrange("p (j t) -> p j t", j=2), ptile.bap[:, 0:256].rearrange("p (j t) -> p j t", j=2), [ptile], [og])
                r0 = 1024 + g * 256
                cx.dma(mixT[r0:r0 + 256, c * 128:(c + 1) * 128].rearrange("(j p) t -> p j t", p=128), og.ap.rearrange("p (j t) -> p j t", j=2),
                       reads=[og], sem_tile=og)

        cx.mark('P2 end')
        cx.barrier()
        cx.top = p1_keep
        KPAD = 1024
        KW = KPAD + 3200
        wqkv = [cx.alloc(f"wqkv{i}", 8 * 384, BF16) for i in range(2)]
        QT = cx.alloc("QT", MIXW, BF16)
        QM = [cx.alloc(f"QM{i}", MIXW, BF16) for i in range(2)]
        KT = cx.alloc("KT", KW, BF16)
        VTt = cx.alloc("VT", KW, BF16)
        qsw = cx.alloc("qsw", 3200, BF16)
        rtmp = cx.alloc("rtmp", 3200, BF16)
        cosf = cx.alloc("cosf", 3200)
        cost = cx.alloc("cos", 3200, BF16)
        sint = cx.alloc("sin", 3200, BF16)
        amf = cx.alloc("amf", 256)
        amask = cx.alloc("amask", 256, BF16)
        hmask = cx.alloc("hmask", 2)
        NVT = 80
        Vaug = cx.alloc("Vaug", NVT * 130, BF16)
        accs = [cx.alloc(f"acc{i}", MIXW) for i in range(2)]
        denb = cx.alloc("denb", MIXW)
        otmp = cx.alloc("otmp", MIXW, BF16)
        Pt = [cx.alloc(f"P{i}", 512, BF16) for i in range(6)]
        cx.dma(cosf.ap, d_cos, writes=[cosf])
        cp("vector", cost.ap, cosf.ap, [cosf], [cost])
        cx.dma(cosf.ap, d_sin, writes=[cosf])
        cp("vector", sint.ap, cosf.ap, [cosf], [sint])
        cx.dma(amf.ap, d_amask[:, 0:256], writes=[amf])
        cx.dma(hmask.ap, d_hmask, writes=[hmask])
        cp("vector", amask.ap, amf.ap, [amf], [amask])
        KT2 = Tile(cosf.ap[:, 0:(KW + 1) // 2].bitcast(BF16)[:, 0:KW], "KT2")
        KT2.w, KT2.r = cosf.w, dict(cosf.r)
        KTs = [KT, KT2]
        for t_ in (KT, KT2, VTt, qsw, otmp, QT):
            memset("vector", t_, t_.ap, 0.0)
        memset("vector", Vaug, Vaug.ap, 1.0)
        for a_ in accs:
            memset("vector", a_, a_.ap, 1.0)
        Vaug4 = Vaug.ap.rearrange("p (k h d) -> p k h d", h=2, d=65)
        Ma = amask.ap[:, 0:128]
        Mb = amask.ap[:, 128:256]

        def qgroups(n):
            k_ = -(-n // 512)
            base_, rem_ = n // k_, n % k_
            gs, s_ = [], 0
            for i_ in range(k_):
                w_ = base_ + (1 if i_ < rem_ else 0)
                gs.append((s_, w_))
                s_ += w_
            return gs

        def rope_dma(T_, off, n):
            for hh in range(2):
                b0 = hh * 64
                cx.dma(qsw.ap[b0:b0 + 8, 0:n], T_.ap[b0 + 8:b0 + 16, off:off + n], reads=[T_], writes=[qsw], sem_tile=qsw)
                cx.dma(qsw.ap[b0 + 8:b0 + 16, 0:n], T_.ap[b0:b0 + 8, off:off + n], reads=[T_], writes=[qsw], sem_tile=qsw)

        def rope_dve(T_, off, n):
            tt("vector", rtmp.ap[0:80, 0:n], qsw.ap[0:80, 0:n], sint.ap[0:80, 0:n], ALU.mult, [qsw, sint], [rtmp])
            tt("vector", T_.ap[0:80, off:off + n], T_.ap[0:80, off:off + n], cost.ap[0:80, 0:n], ALU.mult, [T_, cost], [T_])
            tt("vector", T_.ap[0:80, off:off + n], T_.ap[0:80, off:off + n], rtmp.ap[0:80, 0:n], ALU.add, [T_, rtmp], [T_])

        vt_index = {}
        vt_list = []

        def vt(dil, tstart_tok):
            key = (dil, tstart_tok)
            if key not in vt_index:
                vt_index[key] = len(vt_list)
                vt_list.append(key)
            return vt_index[key]
        sgroups = []
        for dil in (1, 4, 16):
            if dil == 1:
                for j0 in range(0, 16, 4):
                    qt_ = []
                    for j in range(j0, j0 + 4):
                        qt_.append((128 * j, 128, vt(1, 128 * j - 64), vt(1, 128 * j + 64), j == 0))
                    sgroups.append(dict(d=1, q=qt_, kind=("c", j0 * 128)))
            elif dil == 4:
                for r in range(4):
                    qt_ = []
                    for j in range(4):
                        qt_.append((r + 4 * 128 * j, 128, vt(4, r + 4 * (128 * j - 64)), vt(4, r + 4 * (128 * j + 64)), j == 0))
                    sgroups.append(dict(d=4, q=qt_, kind=("s4", r)))
            else:
                for r0 in range(0, 16, 4):
                    qt_ = []
                    for r in range(r0, r0 + 4):
                        qt_.append((r, 128, vt(16, r - 16 * 64), vt(16, r + 16 * 64), True))
                    sgroups.append(dict(d=16, q=qt_, kind=("s16", r0)))
            sgroups.append(dict(d=dil, q=[(2048, 1, vt(dil, 2048 - 64 * dil), vt(dil, 2048 - 63 * dil), False)], kind=("h", dil)))
        assert len(vt_list) <= NVT, len(vt_list)

        pi = 0
        sgi = 0
        norm_pend = []
        def load_wqkv(hp_):
            wq_ = wqkv[hp_ % 2]
            wq3_ = wq_.ap.rearrange("p (c n) -> p c n", c=8)
            for qi_ in range(3):
                c0 = qi_ * 1024 + hp_ * 128
                wload(wq3_[:, :, qi_ * 128:(qi_ + 1) * 128], w_in[:, c0:c0 + 128], 8, wq_)
        load_wqkv(0)
        ppi = [0]

        def prep_closures(hp_):
            cl = []
            wq_ = wqkv[hp_ % 2]
            wq3_ = wq_.ap.rearrange("p (c n) -> p c n", c=8)
            KTn = KTs[hp_ % 2]

            def grp(which, dst, off, s0, w_):
                def f():
                    pp = PS[4 + ppi[0] % 2]
                    ppi[0] += 1
                    for kc in range(8):
                        mm(pp.ap[:, 0:w_], wq3_[:, kc, which * 128:(which + 1) * 128], hT3[:, kc, s0:s0 + w_], kc == 0, kc == 7, [wq_, hT], [pp])
                    cp("scalar" if (ppi[0] % 2) else "vector", dst.ap[:, off + s0:off + s0 + w_], pp.ap[:, 0:w_], [pp], [dst])
                return f
            for (s0, w_) in qgroups(NKV):
                cl.append(grp(1, KTn, KPAD, s0, w_))
            cl.append(lambda: rope_dma(KTn, KPAD, NKV))
            vg = [grp(2, VTt, KPAD, s0, w_) for (s0, w_) in qgroups(NKV)]
            cl.extend(vg[:3])
            cl.append(lambda: rope_dve(KTn, KPAD, NKV))
            cl.extend(vg[3:])
            for (s0, w_) in qgroups(NQ):
                cl.append(grp(0, QT, 0, s0, w_))
            cl.append(lambda: rope_dma(QT, 0, NQ))
            if hp_ < 7:
                cl.append(lambda: load_wqkv(hp_ + 1))
            return cl

        def post_prep(hp_):
            rope_dve(QT, 0, NQ)
            for hh in range(2):
                act(QM[hh].ap[:, 0:NQ], QT.ap[:, 0:NQ], AF.Copy, [QT, hmask], [QM[hh]], scale=hmask.ap[:, hh:hh + 1])
            for v0 in range(0, len(vt_list), 8):
                ptile = PT[(v0 // 8) % 2]
                nv = min(8, len(vt_list) - v0)
                for vi in range(nv):
                    dil, ts_ = vt_list[v0 + vi]
                    c_ = KPAD + ts_
                    tr(ptile.bap[:, vi * 128:(vi + 1) * 128], VTt.ap[:, c_:c_ + 128 * dil:dil], identb.ap, [VTt, identb], [ptile], signal=(vi == nv - 1))
                cp("vector", Vaug4[:, v0:v0 + nv, :, 0:64], ptile.bap[:, 0:nv * 128].rearrange("p (k h d) -> p k h d", h=2, d=64), [ptile], [Vaug])

        for f_ in prep_closures(0):
            f_()
        post_prep(0)
        for hp in range(8):
            KT = KTs[hp % 2]
            nxt = prep_closures(hp + 1) if hp < 7 else []
            if hp == 7:
                wo = Tile(cx.arena[:, persist_top:persist_top + 8192].bitcast(BF16)[:, 0:16 * 1024], "wo")
                wo.w, wo.r = hT.w, dict(hT.r)
                wo3 = wo.ap.rearrange("p (c n) -> p c n", c=16)
                wload(wo3, w_out, 16, wo)
            while norm_pend:
                norm_pend.pop(0)()
            for hh in range(2):
                acc = accs[hh]
                pend = []
                for sg in sgroups:
                    dil = sg["d"]
                    qts = sg["q"]
                    nqt = len(qts)
                    nq = qts[0][1]
                    bA, bB = PS[2 * (sgi % 2)], PS[2 * (sgi % 2) + 1]
                    PA, PB = Pt[2 * (sgi % 2)], Pt[2 * (sgi % 2) + 1]
                    pO = PS[6 + sgi % 2]
                    sgi += 1
                    for qi, (qs, nq_, va, vb, pad) in enumerate(qts):
                        ka = KPAD + vt_list[va][1]
                        kb = KPAD + vt_list[vb][1]
                        rhs = QM[hh].ap[:, qs:qs + nq_ * dil:dil] if nq_ > 1 else QM[hh].ap[:, qs:qs + 1]
                        mm(bA.ap[:, qi * 128:qi * 128 + nq_], KT.ap[:, ka:ka + 128 * dil:dil], rhs, True, True, [KT, QM[hh]], [bA])
                        mm(bB.ap[:, qi * 128:qi * 128 + nq_], KT.ap[:, kb:kb + 128 * dil:dil], rhs, True, True, [KT, QM[hh]], [bB])
                    A3 = bA.ap.rearrange("p (j q) -> p j q", q=128)[:, 0:nqt, 0:nq]
                    B3 = bB.ap.rearrange("p (j q) -> p j q", q=128)[:, 0:nqt, 0:nq]
                    PA3 = PA.ap.rearrange("p (j q) -> p j q", q=128)[:, 0:nqt, 0:nq]
                    PB3 = PB.ap.rearrange("p (j q) -> p j q", q=128)[:, 0:nqt, 0:nq]
                    act(PA3, A3, AF.Exp, [bA], [PA], scale=0.125)
                    act(PB3, B3, AF.Exp, [bB], [PB], scale=0.125)
                    if nq > 1:
                        tt("vector", PA3, PA3, Ma.unsqueeze(1).to_broadcast([128, nqt, 128]), ALU.mult, [PA, amask], [PA])
                        tt("vector", PB3, PB3, Mb.unsqueeze(1).to_broadcast([128, nqt, 128]), ALU.mult, [PB, amask], [PB])
                        for qi, q_ in enumerate(qts):
                            if q_[4]:
                                memset("vector", PA, PA.ap[0:64, qi * 128:(qi + 1) * 128], 0.0)
                    else:
                        tt("vector", PB3, PB3, amask.ap[:, 127:128].unsqueeze(1), ALU.mult, [PB, amask], [PB])

                    def pv(qts=qts, PA=PA, PB=PB, pO=pO, hh=hh, sg=sg, nq=nq, nqt=nqt, acc=acc):
                        for qi, (qs, nq_, va, vb, pad) in enumerate(qts):
                            op("tensor", lambda e, qi=qi, va=va, nq_=nq_: e.matmul(out=pO.ap[0:65, qi * 128:qi * 128 + nq_], lhsT=Vaug4[:, va, hh, :],
                                                                               rhs=PA.ap[:, qi * 128:qi * 128 + nq_], start=True, stop=False),
                               [Vaug, PA], [pO], signal=False)
                            op("tensor", lambda e, qi=qi, vb=vb, nq_=nq_: e.matmul(out=pO.ap[0:65, qi * 128:qi * 128 + nq_], lhsT=Vaug4[:, vb, hh, :],
                                                                               rhs=PB.ap[:, qi * 128:qi * 128 + nq_], start=False, stop=True),
                               [Vaug, PB], [pO], signal=True)
                        kind, arg = sg["kind"]
                        if kind == "c":
                            cp("scalar", acc.ap[0:65, arg:arg + 512], pO.ap[0:65, 0:512], [pO], [acc])
                        elif kind == "s4":
                            av = acc.ap[0:65, arg:arg + 2048:4]
                            tt("vector", av, av, pO.ap[0:65, 0:512], ALU.add, [acc, pO], [acc])
                        elif kind == "s16":
                            av = acc.ap[0:65, 0:2048].rearrange("p (t r) -> p r t", r=16)[:, arg:arg + 4, :]
                            tt("vector", av, av, pO.ap[0:65, 0:512].rearrange("p (r t) -> p r t", r=4), ALU.add, [acc, pO], [acc])
                        else:
                            if arg == 1:
                                cp("vector", acc.ap[0:65, 2048:2049], pO.ap[0:65, 0:1], [pO], [acc])
                            else:
                                tt("vector", acc.ap[0:65, 2048:2049], acc.ap[0:65, 2048:2049], pO.ap[0:65, 0:1], ALU.add, [acc, pO], [acc])
                    pend.append(pv)
                    if len(pend) > 1:
                        pend.pop(0)()
                    if nxt:
                        nxt.pop(0)()
                while pend:
                    pend.pop(0)()
                while norm_pend:
                    norm_pend.pop(0)()
                drow = hp * 2 + hh
                dtile = Tile(None, f"dscr{drow}")
                cx.dma(dscr[drow:drow + 1, :], acc.ap[64:65, :], reads=[acc], writes=[dtile], sem_tile=acc)
                cx.dma(denb.ap[0:64, :], dscr[drow:drow + 1, :].partition_broadcast(64), reads=[dtile], writes=[denb])

                def norm(acc=acc, hp=hp, hh=hh):
                    act(denb.ap[0:64, :], denb.ap[0:64, :], AF.Ln, [denb], [denb])
                    act(denb.ap[0:64, :], denb.ap[0:64, :], AF.Exp, [denb], [denb], scale=-1.0)
                    tt("vector", otmp.ap[0:64, 0:NQ], acc.ap[0:64, 0:NQ], denb.ap[0:64, 0:NQ], ALU.mult, [acc, denb], [otmp])
                    r0 = hp * 128 + hh * 64
                    cx.dma(mixT[r0:r0 + 64, :], otmp.ap[0:64, :], reads=[otmp], sem_tile=otmp)
                norm_pend.append(norm)
            while nxt:
                nxt.pop(0)()
            if hp < 7:
                post_prep(hp + 1)

        while norm_pend:
            norm_pend.pop(0)()
        cx.mark('P3 end')
        cx.barrier()
        cx.top = persist_top
        wo_res = cx.alloc("wo_res", 8192)
        x1 = cx.alloc("x1", 16 * 1024)
        x1T = [Tile(x1.ap[:, i_ * 1024:(i_ + 1) * 1024], f"x1t{i_}") for i_ in range(16)]
        h2T = cx.alloc("h2T", 8 * MIXW, BF16)
        h2T3 = h2T.ap.rearrange("p (c t) -> p c t", c=8)
        nw2 = cx.alloc("nw2", 8)
        p4_keep = cx.top
        x1b = cx.alloc("x1b", 1024)
        mT = cx.alloc("mT", 16 * 1152, BF16)
        xt2 = [cx.alloc(f"xtb{i}", 1024) for i in range(4)]
        xs2 = [cx.alloc(f"xsb{i}", 1024, BF16) for i in range(2)]
        ss2 = [cx.alloc(f"ssb{i}", 1) for i in range(2)]
        rs2 = [cx.alloc(f"rsb{i}", 1) for i in range(2)]
        cx.dma(nw2.ap, d_nw2, writes=[nw2])
        mT3 = mT.ap.rearrange("p (c t) -> p c t", c=16)
        p4fifo = []
        for i in range(17):
            b = i % 2
            if i in (0, 9):
                t0_, nt_ = (0, 1152) if i == 0 else (1152, 1024)
                for kc in range(16):
                    cx.dma(mT3[:, kc, 0:nt_], mixT[kc * 128:(kc + 1) * 128, t0_:t0_ + nt_], writes=[mT], sem_tile=mT)
            lt = i * 128 - (0 if i < 9 else 1152)
            if i == 0:
                for i2 in range(3):
                    cx.dma(xt2[i2].ap, x[i2 * 128:(i2 + 1) * 128, :], writes=[xt2[i2]])
            if i + 3 < 17:
                cx.dma(xt2[(i + 3) % 4].ap, x[(i + 3) * 128:(i + 4) * 128, :], writes=[xt2[(i + 3) % 4]])
            xtc = xt2[i % 4]
            for nh in range(2):
                pp = PS[(2 * i + nh) % 4]
                for kc in range(16):
                    mm(pp.ap, mT3[:, kc, lt:lt + 128], wo3[:, kc, nh * 512:(nh + 1) * 512], kc == 0, kc == 15, [mT, wo], [pp])
                x1t, x1ap = (x1T[i], x1T[i].ap) if i < 16 else (x1b, x1b.ap)
                tt("vector", x1ap[:, nh * 512:(nh + 1) * 512], pp.ap, xtc.ap[:, nh * 512:(nh + 1) * 512], ALU.add, [pp, xtc], [x1t])
            def normtr(x1t=x1t, x1ap=x1ap, i=i, b=b):
                load_norm_transpose(x1t, x1ap, 128, nw2, h2T3, i * 128, xs2[b], rs2[b], ss2[b], PT[b])
                h2T.w = ("vector", cx.sem["vector"], cx.cnt["vector"])
            p4fifo.append(normtr)
            while len(p4fifo) > 1:
                p4fifo.pop(0)()
        while p4fifo:
            p4fifo.pop(0)()

        cx.mark('P4 end')
        cx.barrier()
        cx.top = p4_keep
        aT_top = cx.top
        GS = 8
        aT = cx.alloc("aT", GS * 2048, BF16)
        _save_top = cx.top
        cx.top = persist_top
        wdg = cx.alloc("wdg", GS * 1024, BF16)
        wstage = cx.alloc("wstage", 8 * 256)
        wup = [cx.alloc(f"wup{i}", 8 * 256, BF16) for i in range(2)]
        assert cx.top <= persist_top + 8192, cx.top
        cx.top = _save_top
        rawf = [cx.alloc(f"rawf{i}", 2052, BF16) for i in range(3)]
        gts = [cx.alloc(f"gt{i}", 2048, BF16) for i in range(2)]
        fcw = cx.alloc("fcw", 132)
        fcb = cx.alloc("fcb", 44)
        diagf = [cx.alloc(f"diagf{i}", 6 * 128, BF16) for i in range(3)]
        ssf = cx.alloc("ssf", 16)
        rsf = cx.alloc("rsf", 16)
        nwf = cx.alloc("nwf", 1024)
        cx.dma(nwf.ap, d_nwf, writes=[nwf])
        cx.dma(fcw.ap, d_fcw, writes=[fcw])
        cx.dma(fcb.ap, d_fcb, writes=[fcb])
        for r_ in rawf:
            memset("vector", r_, r_.ap[:, 0:1], 0.0)
        wdg3 = wdg.ap.rearrange("p (c n) -> p c n", c=GS)
        aT3 = aT.ap.rearrange("p (c t) -> p c t", c=GS)
        ws3 = wstage.ap.rearrange("p (c n) -> p c n", c=8)
        pi = 0
        ri = 0
        ci = 0
        fifo = []
        tgroups = [(i_ * 410 - max(0, i_ - 4), 410 if i_ < 4 else 409) for i_ in range(5)]
        assert tgroups[-1][0] + tgroups[-1][1] == 2049 and all(tgroups[i_][0] + tgroups[i_][1] == tgroups[i_ + 1][0] for i_ in range(4))
        groups = [list(range(g0, min(22, g0 + GS))) for g0 in range(0, 22, GS)]
        for grp in groups:
            for il, i in enumerate(grp):
                wu = wup[i % 2]
                wu3 = wu.ap.rearrange("p (c n) -> p c n", c=8)
                dgf = diagf[i % 3]
                gt = gts[i % 2]
                for part in range(2):
                    c0 = part * 2816 + i * 128
                    cx.dma(ws3[:, :, part * 128:(part + 1) * 128], w_up[:, c0:c0 + 128].rearrange("(c p) n -> p c n", p=128), writes=[wstage])
                    ch = part * 22 + i
                    for k in range(3):
                        ts("vector", dgf.ap[:, (part * 3 + k) * 128:(part * 3 + k + 1) * 128], identf.ap, fcw.ap[:, ch * 3 + k:ch * 3 + k + 1], None,
                           ALU.mult, None, [identf, fcw], [dgf])
                cp("vector", wu.ap, wstage.ap, [wstage], [wu])
                cx.dma(wdg3[:, il, :], w_down[i * 128:(i + 1) * 128, :], writes=[wdg], q="gpsimd")
                for part in range(2):
                    raw = rawf[ri % 3]
                    ri += 1
                    for (s0, w_) in tgroups:
                        pp = PS[pi % 4]
                        pi += 1
                        for kc in range(8):
                            mm(pp.ap[:, 0:w_], wu3[:, kc, part * 128:(part + 1) * 128], h2T3[:, kc, s0:s0 + w_], kc == 0, kc == 7, [wu, h2T], [pp])
                        cp("vector" if (pi % 2) else "scalar", raw.ap[:, 1 + s0:1 + s0 + w_], pp.ap[:, 0:w_], [pp], [raw])
                    ch = part * 22 + i

                    def conv(part=part, raw=raw, dgf=dgf, ch=ch, gt=gt, il=il):
                        global_ci = [0]
                        for tg in range(4):
                            pc = PS[4 + (tg % 2)]
                            for k in range(3):
                                mm(pc.ap, dgf.ap[:, (part * 3 + k) * 128:(part * 3 + k + 1) * 128], raw.ap[:, tg * 512 + k:tg * 512 + k + 512], k == 0, k == 2, [dgf, raw], [pc])
                            if part == 0:
                                act(gt.ap[:, tg * 512:(tg + 1) * 512], pc.ap, AF.Silu, [pc, fcb], [gt], bias=fcb.ap[:, ch:ch + 1])
                            else:
                                stt("vector", aT3[:, il, tg * 512:(tg + 1) * 512], pc.ap, fcb.ap[:, ch:ch + 1], gt.ap[:, tg * 512:(tg + 1) * 512], ALU.add, ALU.mult,
                                    [pc, fcb, gt], [aT])
                    fifo.append(conv)
                    while len(fifo) > 1:
                        fifo.pop(0)()
            while fifo:
                fifo.pop(0)()
            last = (grp is groups[-1])
            for t in range(16):
                for nh in range(2):
                    pp = PS[6 + nh]
                    for il in range(len(grp)):
                        mm(pp.ap, aT3[:, il, t * 128:(t + 1) * 128], wdg3[:, il, nh * 512:(nh + 1) * 512], il == 0, il == len(grp) - 1, [aT, wdg], [pp])
                    tt("vector", x1T[t].ap[:, nh * 512:(nh + 1) * 512], pp.ap, x1T[t].ap[:, nh * 512:(nh + 1) * 512],
                       ALU.add, [pp, x1T[t]], [x1T[t]])
                if last:
                    act(junk.ap, x1T[t].ap, AF.Square, [x1T[t]], [junk, ssf], accum_out=ssf.ap[:, t:t + 1])

        cx.mark('P5 end')
        cx.barrier()
        cx.top = aT_top
        ost = [cx.alloc(f"ost{i}", 1024) for i in range(2)]
        act(rsf.ap, ssf.ap, AF.Ln, [ssf, epst], [rsf], scale=1.0 / 1024, bias=epst.ap)
        act(rsf.ap, rsf.ap, AF.Exp, [rsf], [rsf], scale=-0.5)
        evs = []
        for ti in range(16):
            o_ = ost[ti % 2]
            stt("vector", o_.ap, x1T[ti].ap, rsf.ap[:, ti:ti + 1], nwf.ap, ALU.mult, ALU.mult, [x1T[ti], rsf, nwf], [o_])
            evs.append(cx.dma(out[ti * 128:(ti + 1) * 128, :], o_.ap, reads=[o_], sem_tile=o_))
        for ev in evs[-2:]:
            cx.wait_event("sync", ev)
        if debug:
            for k_, ev in cx.dma_last.items():
                cx.wait_event("sync", ev)
        cx.mark('end')
        cx.emit()
        nc._marks = cx.marks
        nc._abs = cx.abs
    return nc


def _consts(flip):
    c = {}
    c["ident"] = np.eye(128, dtype=np.float32)
    s = np.arange(128)[:, None]
    l = np.arange(128)[None, :]
    c["U"] = (s <= l).astype(np.float32)
    c["UsN"] = -(s < l).astype(np.float32)
    c["ones"] = np.ones((128, 128), np.float32)
    sel = np.zeros((4, 512), np.float32)
    for j in range(4):
        sel[j, j * 128:(j + 1) * 128] = 1.0
    c["sel4"] = sel
    mb = np.zeros((128, 256), np.float32)
    mb[:, 0:128] = np.where(s <= l, 0.0, -30000.0)
    mb[:, 128:256] = np.where(s >= l, 0.0, -30000.0)
    c["maskb"] = mb
    am = np.zeros((128, 17 * 128), np.float32)
    am[:, 0:128] = (s >= l).astype(np.float32)
    am[:, 128:256] = (s <= l).astype(np.float32)
    c["amask"] = am
    half = 8
    inv = np.power(np.float32(500000.0), -np.arange(half, dtype=np.float32) * 2.0 / 16).astype(np.float32)
    pos = np.arange(3200, dtype=np.float32)
    ang = pos[None, :] * inv[:, None]
    cs, sn = np.cos(ang).astype(np.float32), np.sin(ang).astype(np.float32)
    if flip:
        sn = -sn
    cosT = np.ones((128, 3200), np.float32)
    sinT = np.zeros((128, 3200), np.float32)
    for b0 in (0, 64):
        cosT[b0:b0 + 8] = cs
        cosT[b0 + 8:b0 + 16] = cs
        sinT[b0:b0 + 8] = -sn
        sinT[b0 + 8:b0 + 16] = sn
    c["ropecos"] = cosT
    c["ropesin"] = sinT
    hm = np.zeros((128, 2), np.float32)
    hm[0:64, 0] = 1.0
    hm[64:128, 1] = 1.0
    c["hmask"] = hm
    return c


def _chunk_layout(v, n):
    v = np.asarray(v, np.float32)
    k = v.shape[1] if v.ndim == 2 else 1
    return np.ascontiguousarray(v.reshape(n, 128, k).transpose(1, 0, 2).reshape(128, n * k))


def _prep(inputs):
    f = lambda a: np.ascontiguousarray(np.asarray(a, np.float32))
    x = f(inputs["x"])
    maps = []
    w_in = f(inputs["w_in"][0])
    w_in_flip = w_in.copy()
    w_in_flip[:, 6144:6160] = w_in[:, 6160:6176]
    w_in_flip[:, 6160:6176] = w_in[:, 6144:6160]
    base = {
        "w_out": f(inputs["w_out"][0]), "w_up": f(inputs["w_up"][0]), "w_down": f(inputs["w_down"][0]),
        "nw1": _chunk_layout(inputs["norm1_w"][0], 8), "nw2": _chunk_layout(inputs["norm2_w"][0], 8),
        "nwf_bc": np.ascontiguousarray(np.broadcast_to(f(inputs["final_norm_w"])[None, :], (128, 1024))),
        "cb": _chunk_layout(inputs["ssm_conv_b"][0], 16), "fcb": _chunk_layout(inputs["ffn_conv_b"][0], 44),
        "dskip_bc": np.ascontiguousarray(np.broadcast_to(f(inputs["d_skip"][0])[None, :], (128, 16))),
        "ssm_nw_bc": np.ascontiguousarray(np.broadcast_to(f(inputs["ssm_norm_w"][0])[None, :], (128, 1024))),
    }
    for flip in (0, 1):
        d = dict(base)
        d.update(_consts(flip))
        cw = f(inputs["ssm_conv_w"][0])
        fcw = f(inputs["ffn_conv_w"][0])
        af, ab = f(inputs["a_log_f"][0]), f(inputs["a_log_b"][0])
        bf_, bb = f(inputs["dt_bias_f"][0]), f(inputs["dt_bias_b"][0])
        if flip:
            cw, fcw = cw[:, ::-1], fcw[:, ::-1]
            af, ab, bf_, bb = ab, af, bb, bf_
        d["cw"] = _chunk_layout(cw, 16)
        d["fcw"] = _chunk_layout(fcw, 44)
        d["alog_bc"] = np.ascontiguousarray(np.broadcast_to(np.concatenate([af, ab])[None, :], (128, 32)))
        d["dtb_bc"] = np.ascontiguousarray(np.broadcast_to(np.concatenate([bf_, bb])[None, :], (128, 32)))
        d["w_in"] = w_in_flip if flip else w_in
        maps.append(d)
    in_maps = []
    for c in range(8):
        b, flip = c // 2, c % 2
        d = dict(maps[flip])
        d["x"] = np.ascontiguousarray(x[b, ::-1]) if flip else np.ascontiguousarray(x[b])
        in_maps.append(d)
    return in_maps


def kernel(**inputs):
    in_maps = _prep(inputs)
    nc = build_nc()
    res = run_bass_kernel_spmd(nc, in_maps, core_ids=list(range(8)))
    outf = np.zeros((4, 4096, 1024), np.float32)
    for c in range(8):
        o = np.asarray(res.results[c]["out"], np.float32)
        b, flip = c // 2, c % 2
        if flip:
            outf[b, 2048:] = o[::-1]
        else:
            outf[b, :2048] = o
    return outf
```

```python
import numpy as np
import ml_dtypes
from contextlib import ExitStack
import concourse.bass as bass
import concourse.mybir as mybir
from concourse.bass_utils import run_bass_kernel_spmd

F32 = mybir.dt.float32
BF16 = mybir.dt.bfloat16
AF = mybir.ActivationFunctionType
ALU = mybir.AluOpType

EPS = 1e-6
NTOK = 4096
OWN = 2048
NQ = 2049
NKV = 3073
NCH = 17
MIXW = 2176


class Tile:
    __slots__ = ("ap", "name", "w", "r", "dsem", "dcnt", "bap")

    def __init__(self, ap, name):
        self.ap = ap
        self.name = name
        self.w = None
        self.r = {}
        self.dsem = None
        self.dcnt = 0


class Ctx:
    ENGS = ("tensor", "vector", "scalar", "gpsimd", "sync")

    def __init__(self, nc, stack):
        self.nc = nc
        self.stack = stack
        self.lists = {e: [] for e in self.ENGS}
        self.sem = {}
        self.cnt = {e: 0 for e in self.ENGS}
        self.seen = {e: {} for e in self.ENGS}
        for e in self.ENGS:
            self.sem[e] = stack.enter_context(nc.semaphore("s_" + e))
        self.nsem = 0
        self.arena = None
        self.top = 0
        self.cap = 0
        self.uid = 0
        self.dma_last = {}
        self.marks = []
        self.abs = {e: [] for e in self.ENGS}

    def mark(self, name):
        self.marks.append((name, dict(self.cnt)))

    def barrier(self):
        evs = [(e, self.sem[e], self.cnt[e]) for e in self.ENGS if self.cnt[e] > 0]
        evs += list(self.dma_last.values())
        for e in self.ENGS:
            for ev in evs:
                if ev[0] != e:
                    self.wait_event(e, ev)

    def init_arena(self, ncols):
        self.arena = self.stack.enter_context(self.nc.sbuf_tensor("arena", [128, ncols], F32))
        self.cap = ncols
        self.top = 0

    def alloc(self, name, cols, dtype=F32):
        n32 = cols if dtype == F32 else (cols + 1) // 2
        assert self.top + n32 <= self.cap, f"arena overflow at {name}: {self.top}+{n32}>{self.cap}"
        ap = self.arena[:, self.top:self.top + n32]
        self.top += n32
        if dtype != F32:
            ap = ap.bitcast(dtype)[:, 0:cols]
        self.uid += 1
        return Tile(ap, f"{name}_{self.uid}")

    def psum(self, name, cols=512, dtype=F32):
        t = self.stack.enter_context(self.nc.psum_tensor(name, [128, cols], dtype))
        return Tile(t[:, :], name)

    def newsem(self, name):
        self.nsem += 1
        return self.stack.enter_context(self.nc.semaphore(f"d{self.nsem}"))

    def _waits(self, eng, reads, writes, skip_own):
        need = {}

        def add(ev):
            if ev is None:
                return
            k, s, v = ev
            if skip_own and k == eng:
                return
            if need.get(k, (None, 0))[1] < v:
                need[k] = (s, v)
        for t in reads:
            add(t.w)
        for t in writes:
            add(t.w)
            for ev in t.r.values():
                add(ev)
        for k, (s, v) in need.items():
            if self.seen[eng].get(k, 0) < v:
                self.seen[eng][k] = v
                self.lists[eng].append(lambda e, s=s, v=v: e.wait_ge(s, v))
                self.abs[eng].append(('w', k, v))

    def op(self, eng, fn, reads=(), writes=(), signal=True):
        self._waits(eng, reads, writes, skip_own=(eng == "tensor"))
        sem = self.sem[eng]
        if signal:
            self.cnt[eng] += 1
            self.lists[eng].append(lambda e, fn=fn, sem=sem: fn(e).then_inc(sem, 1))
            self.abs[eng].append(('i', eng, 1))
            ev = (eng, sem, self.cnt[eng])
        else:
            self.lists[eng].append(lambda e, fn=fn: fn(e))
            ev = (eng, sem, self.cnt[eng] + 1)
        for t in reads:
            if t.r.get(eng, (None, None, 0))[2] < ev[2]:
                t.r[eng] = ev
        for t in writes:
            t.w = ev
            t.r = {}
        return ev

    def dma(self, out_ap, in_ap, reads=(), writes=(), q="sync", sem_tile=None):
        st = sem_tile or (writes[0] if writes else reads[0])
        if st.dsem is None:
            st.dsem = self.newsem(st.name)
        key = "dma:" + st.name
        need_w = []
        for t in writes:
            if t.w is not None and t.w[0] == key and not t.r:
                continue
            need_w.append(t)
        self._waits(q, reads, need_w, skip_own=False)
        st.dcnt += 16
        sem = st.dsem
        ev = (key, sem, st.dcnt)
        self.lists[q].append(lambda e, o=out_ap, i=in_ap, sem=sem: e.dma_start(out=o, in_=i).then_inc(sem, 16))
        self.abs[q].append(('i', key, 16))
        self.dma_last[key] = ev
        for t in reads:
            t.r[key] = ev
        for t in writes:
            t.w = ev
            t.r = {}
        return ev

    def wait_event(self, eng, ev):
        k, s, v = ev
        if self.seen[eng].get(k, 0) < v:
            self.seen[eng][k] = v
            self.lists[eng].append(lambda e, s=s, v=v: e.wait_ge(s, v))
            self.abs[eng].append(('w', k, v))

    def emit(self):
        with self.nc.Block() as block:
            @block.sync
            def _(e):
                for f in self.lists["sync"]:
                    f(e)

            @block.tensor
            def _(e):
                for f in self.lists["tensor"]:
                    f(e)

            @block.vector
            def _(e):
                for f in self.lists["vector"]:
                    f(e)

            @block.scalar
            def _(e):
                for f in self.lists["scalar"]:
                    f(e)

            @block.gpsimd
            def _(e):
                for f in self.lists["gpsimd"]:
                    f(e)


def build_nc(debug=False):
    nc = bass.Bass("TRN2", target_bir_lowering=False)

    def din(name, shape, dt=F32):
        return nc.dram_tensor(name, list(shape), dt, kind="ExternalInput").ap()

    x = din("x", [NTOK, 1024])
    w_in = din("w_in", [1024, 6176])
    w_out = din("w_out", [2048, 1024])
    w_up = din("w_up", [1024, 5632])
    w_down = din("w_down", [2816, 1024])
    d_nw1 = din("nw1", [128, 8])
    d_nw2 = din("nw2", [128, 8])
    d_nwf = din("nwf_bc", [128, 1024])
    d_cw = din("cw", [128, 48])
    d_cb = din("cb", [128, 16])
    d_fcw = din("fcw", [128, 132])
    d_fcb = din("fcb", [128, 44])
    d_alog = din("alog_bc", [128, 32])
    d_dtb = din("dtb_bc", [128, 32])
    d_dsk = din("dskip_bc", [128, 16])
    d_snw = din("ssm_nw_bc", [128, 1024])
    d_ident = din("ident", [128, 128])
    d_U = din("U", [128, 128])
    d_UsN = din("UsN", [128, 128])
    d_ones = din("ones", [128, 128])
    d_sel4 = din("sel4", [4, 512])
    d_maskb = din("maskb", [128, 256])
    d_amask = din("amask", [128, 17 * 128])
    d_cos = din("ropecos", [128, 3200])
    d_sin = din("ropesin", [128, 3200])
    d_hmask = din("hmask", [128, 2])
    out = nc.dram_tensor("out", [OWN, 1024], F32, kind="ExternalOutput").ap()
    mixT = nc.dram_tensor("mixT", [2048, MIXW], BF16, kind=("ExternalOutput" if debug else "Internal")).ap()

    with ExitStack() as st:
        cx = Ctx(nc, st)
        cx.init_arena(53200)
        op = cx.op

        def act(out_, in_, func, reads, writes, **kw):
            return op("scalar", lambda e: e.activation(out=out_, in_=in_, func=func, **kw), reads, writes)

        def tt(eng, out_, in0, in1, o, reads, writes):
            return op(eng, lambda e: e.tensor_tensor(out=out_, in0=in0, in1=in1, op=o), reads, writes)

        def ts(eng, out_, in0, s1, s2, o0, o1, reads, writes):
            if s2 is None:
                return op(eng, lambda e: e.tensor_scalar(out=out_, in0=in0, scalar1=s1, scalar2=None, op0=o0), reads, writes)
            return op(eng, lambda e: e.tensor_scalar(out=out_, in0=in0, scalar1=s1, scalar2=s2, op0=o0, op1=o1), reads, writes)

        def stt(eng, out_, in0, sc, in1, o0, o1, reads, writes):
            return op(eng, lambda e: e.scalar_tensor_tensor(out=out_, in0=in0, scalar=sc, in1=in1, op0=o0, op1=o1), reads, writes)

        def cp(eng, out_, in_, reads, writes):
            if eng == "scalar":
                return act(out_, in_, AF.Copy, reads, writes)
            return op(eng, lambda e: e.tensor_copy(out=out_, in_=in_), reads, writes)

        def mm(out_, lhsT, rhs, start, stop, reads, writes):
            return op("tensor", lambda e: e.matmul(out=out_, lhsT=lhsT, rhs=rhs, start=start, stop=stop), reads, writes, signal=stop)

        def tr(out_, in_, ident, reads, writes, signal=True):
            return op("tensor", lambda e: e.transpose(out=out_, in_=in_, identity=ident), reads, writes, signal=signal)

        def memset(eng, t, ap, val):
            return op(eng, lambda e: e.memset(ap, val), (), [t])

        def bc(ap2, n):
            return ap2.unsqueeze(2).to_broadcast([ap2.shape[0], ap2.shape[1], n])

        identf = cx.alloc("identf", 128)
        identb = cx.alloc("identb", 128, BF16)
        epst = cx.alloc("eps", 1)
        onet = cx.alloc("one", 1)
        junk = cx.alloc("junk", 1024, BF16)
        cx.dma(identf.ap, d_ident, writes=[identf])
        cp("vector", identb.ap, identf.ap, [identf], [identb])
        memset("vector", epst, epst.ap, EPS)
        memset("vector", onet, onet.ap, 1.0)
        PS = [cx.psum(f"ps{i}", 512, F32) for i in range(8)]
        for t_ in PS:
            t_.bap = t_.ap.bitcast(BF16)
        PT = [PS[6], PS[7]]
        persist_top = cx.top
        hT = cx.alloc("hT", 8 * NTOK, BF16)
        hT3 = hT.ap.rearrange("p (c t) -> p c t", c=8)
        mixtile = Tile(None, "mixtile")
        dscr = nc.dram_tensor("dscr", [16, MIXW], F32, kind="Internal").ap()

        def wload(dst3, src, nck, tile_, reads=()):
            for kc_ in range(nck):
                cx.dma(dst3[:, kc_, :], src[kc_ * 128:(kc_ + 1) * 128, :], reads=list(reads), writes=[tile_], q="gpsimd")

        def load_norm_transpose(xt_tile, xap, rows, nw_t, dst3, col0, xs_t, rs_t, ss_t, ptile, dve_scale=False):
            act(junk.ap[0:rows, :], xap[0:rows, :], AF.Square, [xt_tile], [junk, ss_t], accum_out=ss_t.ap[0:rows, :])
            act(rs_t.ap[0:rows, :], ss_t.ap[0:rows, :], AF.Ln, [ss_t, epst], [rs_t], scale=1.0 / 1024, bias=epst.ap[0:rows, :])
            act(rs_t.ap[0:rows, :], rs_t.ap[0:rows, :], AF.Exp, [rs_t], [rs_t], scale=-0.5)
            if dve_scale:
                ts("vector", xs_t.ap[0:rows, :], xap[0:rows, :], rs_t.ap[0:rows, :], None, ALU.mult, None, [xt_tile, rs_t], [xs_t])
            else:
                act(xs_t.ap[0:rows, :], xap[0:rows, :], AF.Copy, [xt_tile, rs_t], [xs_t], scale=rs_t.ap[0:rows, :])
            for kc in range(8):
                tr(ptile.bap[:, kc * 128:kc * 128 + rows], xs_t.ap[0:rows, kc * 128:(kc + 1) * 128], identb.ap[0:rows, 0:rows],
                   [xs_t, identb], [ptile], signal=(kc == 7))
            tt("vector", dst3[:, :, col0:col0 + rows], ptile.bap.rearrange("p (c t) -> p c t", c=8)[:, :, 0:rows],
               bc(nw_t.ap, rows), ALU.mult, [ptile, nw_t], [])

        nw1 = cx.alloc("nw1", 8)
        cx.dma(nw1.ap, d_nw1, writes=[nw1])
        p1_keep = cx.top
        xt = [cx.alloc(f"xt{i}", 1024) for i in range(2)]
        xs = [cx.alloc(f"xs{i}", 1024, BF16) for i in range(2)]
        ssq = [cx.alloc(f"ss{i}", 1) for i in range(2)]
        rsq = [cx.alloc(f"rs{i}", 1) for i in range(2)]
        for i in range(32):
            b = i % 2
            cx.dma(xt[b].ap, x[i * 128:(i + 1) * 128, :], writes=[xt[b]])
            load_norm_transpose(xt[b], xt[b].ap, 128, nw1, hT3, i * 128, xs[b], rsq[b], ssq[b], PT[b], dve_scale=True)
            hT.w = (("vector"), cx.sem["vector"], cx.cnt["vector"])
        p1_top = cx.top

        cx.mark('P1 end')
        cx.barrier()
        cx.top = p1_keep
        wdt = cx.alloc("wdt", 8 * 32, BF16)
        wload(wdt.ap.rearrange("p (c n) -> p c n", c=8), w_in[:, 6144:6176], 8, wdt)
        alog = cx.alloc("alog", 32)
        dtb = cx.alloc("dtb", 32)
        dsk = cx.alloc("dsk", 16)
        snw = cx.alloc("snw", 1024)
        Ut = cx.alloc("U", 128)
        UsN = cx.alloc("UsN", 128)
        onesf = cx.alloc("onesf", 128)
        sel4 = cx.alloc("sel4", 512)
        maskbf = cx.alloc("maskbf", 256)
        maskb = cx.alloc("maskb", 256, BF16)
        cwt = cx.alloc("cw", 48)
        cbt = cx.alloc("cb", 16)
        for t_, d_ in ((alog, d_alog), (dtb, d_dtb), (dsk, d_dsk), (snw, d_snw), (Ut, d_U), (UsN, d_UsN), (onesf, d_ones),
                       (maskbf, d_maskb), (cwt, d_cw), (cbt, d_cb)):
            cx.dma(t_.ap, d_, writes=[t_])
        cx.dma(sel4.ap[0:4, :], d_sel4, writes=[sel4])
        cp("vector", maskb.ap, maskbf.ap, [maskbf], [maskb])
        DT = cx.alloc("DT", 1024)
        AA = cx.alloc("AA", 1024)
        NEGV = cx.alloc("NEGV", 1024)
        EY = cx.alloc("EY", 1024)
        WW = cx.alloc("WW", 1024)
        DEC = cx.alloc("DEC", 1024)
        A3s = [cx.alloc(f"Asp{i}", 1024, BF16) for i in range(3)]
        Ub = cx.alloc("Ub", 128, BF16)
        UsNb = cx.alloc("UsNb", 128, BF16)
        onesb = cx.alloc("onesb", 128, BF16)
        sel4b = cx.alloc("sel4b", 512, BF16)
        vsp = [cx.alloc(f"vsp{i}", 256, BF16) for i in range(3)]
        vR = cx.alloc("vR", 256)
        wg = cx.alloc("wg", 8 * 512, BF16)
        wz = cx.alloc("wz", 8 * 256, BF16)

        def load_wg(g_):
            cols = [(4096 + g_ * 256, 256), (4096 + 1024 + g_ * 128, 128), (4096 + 1536 + g_ * 128, 128)]
            off = 0
            for (c0, wd_) in cols:
                wload(wg.ap.rearrange("p (c n) -> p c n", c=8)[:, :, off:off + wd_], w_in[:, c0:c0 + wd_], 8, wg)
                off += wd_

        def load_wz(g_):
            wload(wz.ap.rearrange("p (c n) -> p c n", c=8), w_in[:, 3072 + g_ * 256:3072 + (g_ + 1) * 256], 8, wz)
        load_wg(0)
        load_wz(0)
        pre_top = cx.top
        CS = cx.alloc("CS", 1024)
        TT = cx.alloc("TT", 1024)
        VV = cx.alloc("VV", 1024)

        def v3(t):
            return t.ap.rearrange("p (c j) -> p c j", j=32)
        for cg in range(2):
            pp = PS[cg]
            for ci in range(16):
                c = cg * 16 + ci
                for kc in range(8):
                    mm(pp.ap[:, ci * 32:(ci + 1) * 32], hT3[:, kc, c * 128:(c + 1) * 128], wdt.ap[:, kc * 32:(kc + 1) * 32],
                       kc == 0, kc == 7, [hT, wdt], [pp])
            tt("vector", DT.ap[:, cg * 512:(cg + 1) * 512].rearrange("p (c j) -> p c j", j=32), pp.ap.rearrange("p (c j) -> p c j", j=32),
               dtb.ap.unsqueeze(1).to_broadcast([128, 16, 32]), ALU.add, [pp, dtb], [DT])
        act(DT.ap, DT.ap, AF.Exp, [DT], [DT])
        ts("vector", CS.ap, DT.ap, 1.0, None, ALU.add, None, [DT], [CS])
        act(DT.ap, CS.ap, AF.Ln, [CS], [DT])
        for _it in range(1):
            act(TT.ap, DT.ap, AF.Exp, [DT], [TT], scale=-1.0)
            tt("vector", TT.ap, TT.ap, CS.ap, ALU.mult, [TT, CS], [TT])
            stt("vector", DT.ap, TT.ap, -1.0, DT.ap, ALU.add, ALU.add, [TT, DT], [DT])
        act(alog.ap, alog.ap, AF.Exp, [alog], [alog])
        ts("vector", alog.ap, alog.ap, -1.0, None, ALU.mult, None, [alog], [alog])
        tt("vector", v3(AA), v3(DT), alog.ap.unsqueeze(1).to_broadcast([128, 32, 32]), ALU.mult, [DT, alog], [AA])
        cp("vector", Ub.ap, Ut.ap, [Ut], [Ub])
        cp("vector", UsNb.ap, UsN.ap, [UsN], [UsNb])
        cp("vector", onesb.ap, onesf.ap, [onesf], [onesb])
        cp("vector", sel4b.ap[0:4, :], sel4.ap[0:4, :], [sel4], [sel4b])
        cp("vector", A3s[0].ap, AA.ap, [AA], [A3s[0]])
        tt("vector", VV.ap, AA.ap, A3s[0].ap, ALU.subtract, [AA, A3s[0]], [VV])
        cp("vector", A3s[1].ap, VV.ap, [VV], [A3s[1]])
        tt("vector", VV.ap, VV.ap, A3s[1].ap, ALU.subtract, [VV, A3s[1]], [VV])
        cp("vector", A3s[2].ap, VV.ap, [VV], [A3s[2]])
        for cg in range(2):
            p1_, p2_ = PS[2 + cg], PS[4 + cg]
            for ci in range(16):
                c = cg * 16 + ci
                for k_ in range(3):
                    mm(p1_.ap[:, ci * 32:(ci + 1) * 32], Ub.ap, A3s[k_].ap[:, c * 32:(c + 1) * 32], k_ == 0, k_ == 2, [Ub, A3s[k_]], [p1_])
                for k_ in range(3):
                    mm(p2_.ap[:, ci * 32:(ci + 1) * 32], onesb.ap, A3s[k_].ap[:, c * 32:(c + 1) * 32], k_ == 0, k_ == 2, [onesb, A3s[k_]], [p2_])
            cp("vector", CS.ap[:, cg * 512:(cg + 1) * 512], p1_.ap, [p1_], [CS])
            cp("scalar", TT.ap[:, cg * 512:(cg + 1) * 512], p2_.ap, [p2_], [TT])
        cp("vector", v3(VV)[:, :, 0:16], v3(CS)[:, :, 0:16], [CS], [VV])
        tt("vector", v3(VV)[:, :, 16:32], v3(AA)[:, :, 16:32], v3(CS)[:, :, 16:32], ALU.subtract, [AA, CS], [VV])
        ts("vector", NEGV.ap, VV.ap, -1.0, None, ALU.mult, None, [VV], [NEGV])
        cp("vector", v3(EY)[:, :, 0:16], v3(VV)[:, :, 0:16], [VV], [EY])
        tt("vector", v3(EY)[:, :, 16:32], v3(TT)[:, :, 16:32], v3(VV)[:, :, 16:32], ALU.add, [TT, VV], [EY])
        act(EY.ap, EY.ap, AF.Exp, [EY], [EY])
        tt("vector", v3(WW)[:, :, 0:16], v3(TT)[:, :, 0:16], v3(VV)[:, :, 0:16], ALU.subtract, [TT, VV], [WW])
        cp("vector", v3(WW)[:, :, 16:32], v3(NEGV)[:, :, 16:32], [NEGV], [WW])
        act(WW.ap, WW.ap, AF.Exp, [WW], [WW])
        tt("vector", WW.ap, WW.ap, DT.ap, ALU.mult, [WW, DT], [WW])
        act(DEC.ap, TT.ap, AF.Exp, [TT], [DEC])

        cx.mark('P2 pre end')
        cx.barrier()
        cx.top = pre_top
        _raw0 = cx.alloc("raw0", 4098, BF16)
        rawb = [_raw0, _raw0]
        acttile = [cx.alloc(f"actt{i}", 512, BF16) for i in range(2)]
        xtok = cx.alloc("xtok", 32 * 256, BF16)
        btok = cx.alloc("btok", 32 * 128, BF16)
        BTt = cx.alloc("BT", 2560, BF16)
        CTt = cx.alloc("CT", 2560, BF16)
        ypart = cx.alloc("ypart", NCH * 256, BF16)
        Sloc = cx.alloc("Sloc", NCH * 256, BF16)
        diag = cx.alloc("diag", 12 * 128, BF16)
        Sst = cx.alloc("S", 256)
        Sbf = cx.alloc("Sbf", 256, BF16)
        Xw = [cx.alloc(f"Xw{i}", 256, BF16) for i in range(2)]
        vT = cx.alloc("vT", 256)
        vT2 = [vT, cx.alloc("vTb", 256)]
        SC = cx.alloc("SC", 512)
        vhl = [(cx.alloc(f"vh{i}", 256, BF16), cx.alloc(f"vl{i}", 256, BF16)) for i in range(2)]
        XV = [cx.alloc(f"XV{i}", 1024, BF16) for i in range(2)]
        Lexp = [cx.alloc(f"Lexp{i}", 128, BF16) for i in range(3)]
        MT = [cx.alloc(f"MT{i}", 128, BF16) for i in range(3)]
        t1s = [cx.alloc(f"t1_{i}", 256) for i in range(2)]
        t1 = t1s[0]
        t2 = cx.alloc("t2", 256)
        szt = cx.alloc("sz", 256)
        sst = cx.alloc("ssg", NCH)
        rstg = cx.alloc("rstg", NCH)
        ostg = [cx.alloc(f"ostg{i}", 256, BF16) for i in range(2)]
        ot = cx.alloc("ot", 256, BF16)
        memset("vector", _raw0, _raw0.ap, 0.0)
        xtok3 = xtok.ap.rearrange("p (c n) -> p c n", n=256)
        btok3 = btok.ap.rearrange("p (c n) -> p c n", n=128)
        EY3, WW3, DEC3, DT3 = v3(EY), v3(WW), v3(DEC), v3(DT)

        def hb(ap2):
            return bc(ap2, 64)

        def r3(ap):
            return ap.rearrange("p (j d) -> p j d", d=64)

        for g in range(4):
            if g > 0:
                load_wz(g)
            chunk_ids = [2 * g, 2 * g + 1, 8 + g, 12 + g]
            for jj in range(4):
                for k in range(3):
                    ts("vector", diag.ap[:, (jj * 3 + k) * 128:(jj * 3 + k + 1) * 128], identf.ap,
                       cwt.ap[:, chunk_ids[jj] * 3 + k:chunk_ids[jj] * 3 + k + 1], None, ALU.mult, None, [identf, cwt], [diag])
            wg3 = wg.ap.rearrange("p (c n) -> p c n", c=8)
            bcount = [0]
            pi = 0
            fifoB, fifoC = [], []
            bi = 0
            for jj in range(4):
                ntok = NTOK if jj < 3 else 2560
                raw = rawb[0]
                ngr = ntok // 512
                if jj == 3:
                    while fifoB:
                        fifoB.pop(0)()
                for tg in range(ngr):
                    pp = PS[pi % 3]
                    pi += 1
                    for kc in range(8):
                        mm(pp.ap, wg3[:, kc, jj * 128:(jj + 1) * 128], hT3[:, kc, tg * 512:(tg + 1) * 512], kc == 0, kc == 7, [wg, hT], [pp])
                    cp("vector", raw.ap[:, 1 + tg * 512:1 + (tg + 1) * 512], pp.ap, [pp], [raw])
                    if jj == 3 and tg == ngr - 1:
                        memset("vector", raw, raw.ap[:, 2561:2562], 0.0)

                    def stageB(jj=jj, tg=tg, raw=raw):
                        nonlocal_bi = stageB.counter[0]
                        stageB.counter[0] += 1
                        pc = PS[3 + (nonlocal_bi % 3)]
                        for k in range(3):
                            mm(pc.ap, diag.ap[:, (jj * 3 + k) * 128:(jj * 3 + k + 1) * 128], raw.ap[:, tg * 512 + k:tg * 512 + k + 512],
                               k == 0, k == 2, [diag, raw], [pc])
                        bias = cbt.ap[:, chunk_ids[jj]:chunk_ids[jj] + 1]
                        if jj == 3:
                            act(CTt.ap[:, tg * 512:(tg + 1) * 512], pc.ap, AF.Silu, [pc, cbt], [CTt], bias=bias)
                            return
                        if jj == 2 and tg < 5:
                            srct, src = BTt, BTt.ap[:, tg * 512:(tg + 1) * 512]
                        else:
                            srct = acttile[nonlocal_bi % 2]
                            src = srct.ap
                        act(src, pc.ap, AF.Silu, [pc, cbt], [srct], bias=bias)

                        def stageC(jj=jj, tg=tg, srct=srct, src=src, k_=nonlocal_bi):
                            ptile = PT[k_ % 2]
                            for sb in range(4):
                                tr(ptile.bap[:, sb * 128:(sb + 1) * 128], src[:, sb * 128:(sb + 1) * 128], identb.ap, [srct, identb], [ptile], signal=(sb == 3))
                            pv = ptile.bap[:, 0:512].rearrange("p (c n) -> p c n", n=128)
                            if jj < 2:
                                cp("vector", xtok3[:, tg * 4:(tg + 1) * 4, jj * 128:(jj + 1) * 128], pv, [ptile], [xtok])
                            else:
                                cp("vector", btok3[:, tg * 4:(tg + 1) * 4, :], pv, [ptile], [btok])
                        fifoC.append(stageC)
                        while len(fifoC) > 1:
                            fifoC.pop(0)()
                    stageB.counter = bcount
                    fifoB.append(stageB)
                    while len(fifoB) > 2:
                        fifoB.pop(0)()
            while fifoB:
                fifoB.pop(0)()
            while fifoC:
                fifoC.pop(0)()

            cx.mark(f'S1 end g{g}')
            if g < 3:
                load_wg(g + 1)
            memset("vector", Sst, Sst.ap, 0.0)
            memset("vector", Sbf, Sbf.ap, 0.0)
            fifo = []
            li = 0
            SC3 = SC.ap.rearrange("p (c v) -> p c v", v=16)
            for vi_, (src3, c0_) in enumerate(((DT3, g * 4), (DT3, 16 + g * 4), (WW3, g * 4), (WW3, 16 + g * 4))):
                cp("vector", SC3[:, :, vi_ * 4:(vi_ + 1) * 4], src3[:, :, c0_:c0_ + 4], [DT, WW], [SC])
            for c in range(31, -1, -1):
                own = c < NCH
                tok = c * 128
                b_ = c % 2
                hc = 16 + g * 4
                xv = XV[b_]
                xv4 = xv.ap.rearrange("p (v j d) -> p v j d", v=4, d=64)
                xw_ap = xv.ap[:, 768:1024]
                if own:
                    tt("vector", xv4, r3(xtok3[:, c, :]).unsqueeze(1).to_broadcast([128, 4, 4, 64]),
                       SC3[:, c, :].rearrange("p (v j) -> p v j", v=4).unsqueeze(3).to_broadcast([128, 4, 4, 64]), ALU.mult, [xtok, SC], [xv])
                    xd_ap = (xv.ap[:, 0:256], xv.ap[:, 256:512])
                    xwf_ap = xv.ap[:, 512:768]
                else:
                    tt("vector", r3(xw_ap), r3(xtok3[:, c, :]), hb(WW3[:, c, hc:hc + 4]), ALU.mult, [xtok, WW], [xv])
                if own:
                    pGV = PS[3 + b_]
                    pY = PS[5]
                    pOt = PS[6]
                    mm(pGV.ap[:, 0:128], BTt.ap[:, tok:tok + 128], CTt.ap[:, tok:tok + 128], True, True, [BTt, CTt], [pGV])
                    vTc = vT2[b_]
                    mm(pGV.ap[0:4, 128:256], AA.ap[:, c * 32 + g * 4:c * 32 + g * 4 + 4], Ut.ap, True, True, [AA, Ut], [pGV])
                    mm(pGV.ap[0:4, 256:384], AA.ap[:, c * 32 + 16 + g * 4:c * 32 + 16 + g * 4 + 4], UsN.ap, True, True, [AA, UsN], [pGV])
                    cp("vector", vTc.ap[0:4, :], pGV.ap[0:4, 128:384], [pGV], [vTc])
                    vh, vl = vhl[b_]
                    cp("scalar", vh.ap[0:4, :], pGV.ap[0:4, 128:384], [pGV], [vh])
                    tt("vector", vl.ap[0:4, :], vTc.ap[0:4, :], vh.ap[0:4, :], ALU.subtract, [vTc, vh], [vl])
                    mm(pOt.ap[:, 0:256], CTt.ap[:, tok:tok + 128], Sbf.ap, True, True, [CTt, Sbf], [pOt])
                    t1c = t1s[b_]
                    tt("vector", r3(t1c.ap), r3(pOt.ap[:, 0:256]), hb(EY3[:, c, hc:hc + 4]), ALU.mult, [pOt, EY], [t1c])
                    mm(pOt.ap[:, 256:512], btok3[:, c, :], xwf_ap, True, True, [btok, xv], [pOt])
                    cp("scalar", Sloc.ap[:, c * 256:(c + 1) * 256], pOt.ap[:, 256:512], [pOt], [Sloc])
                if c > 0:
                    pS = PS[7]
                    mm(pS.ap[:, 0:256], btok3[:, c, :], xw_ap, True, True, [btok, xv], [pS])
                    tt("vector", r3(Sst.ap), r3(Sst.ap), hb(DEC3[:, c, hc:hc + 4]), ALU.mult, [Sst, DEC], [Sst])
                    tt("vector", Sst.ap, Sst.ap, pS.ap[:, 0:256], ALU.add, [Sst, pS], [Sst])
                    cp("scalar", Sbf.ap, Sst.ap, [Sst], [Sbf])
                if own:
                    for j in range(4):
                        for d_ in range(2):
                            pL = PS[li % 3]
                            lx, mt = Lexp[li % 3], MT[li % 3]
                            li += 1
                            mm(pL.ap[:, 0:128], sel4b.ap[0:4, j * 128:(j + 1) * 128], vh.ap[0:4, d_ * 128:(d_ + 1) * 128], True, False, [sel4b, vh], [pL])
                            mm(pL.ap[:, 0:128], sel4b.ap[0:4, j * 128:(j + 1) * 128], vl.ap[0:4, d_ * 128:(d_ + 1) * 128], False, False, [sel4b, vl], [pL])
                            mm(pL.ap[:, 0:128], identb.ap, maskb.ap[:, d_ * 128:(d_ + 1) * 128], False, True, [identb, maskb], [pL])
                            col = c * 32 + d_ * 16 + g * 4 + j
                            act(lx.ap, pL.ap[:, 0:128], AF.Exp, [pL, NEGV], [lx], bias=NEGV.ap[:, col:col + 1])
                            tt("vector", mt.ap, lx.ap, pGV.ap[:, 0:128], ALU.mult, [lx, pGV], [mt])

                            def ymm(j=j, d_=d_, mt=mt, xd_ap=xd_ap, xv=xv, pY=pY):
                                op("tensor", lambda e: e.matmul(out=pY.ap[:, j * 64:(j + 1) * 64], lhsT=mt.ap, rhs=xd_ap[d_][:, j * 64:(j + 1) * 64],
                                                                start=(d_ == 0), stop=(d_ == 1)),
                                   [mt, xv], [pY], signal=(d_ == 1 and j == 3))
                            fifo.append(ymm)
                            while len(fifo) > 2:
                                fifo.pop(0)()

                    def tail(c=c, pY=pY, t1c=t1c):
                        tt("vector", t2.ap, t1c.ap, pY.ap[:, 0:256], ALU.add, [t1c, pY], [t2])
                        stt("vector", r3(ypart.ap[:, c * 256:(c + 1) * 256]), r3(xtok3[:, c, :]), 1.0, hb(dsk.ap[:, g * 4:g * 4 + 4]), ALU.mult, ALU.mult,
                            [xtok, dsk], [ypart])
                        tt("vector", ypart.ap[:, c * 256:(c + 1) * 256], ypart.ap[:, c * 256:(c + 1) * 256], t2.ap, ALU.add, [ypart, t2], [ypart])
                    fifo.append(tail)
            while fifo:
                fifo.pop(0)()

            cx.mark(f'passA end g{g}')
            memset("vector", Sst, Sst.ap, 0.0)
            memset("vector", Sbf, Sbf.ap, 0.0)
            Sbfs = [Sbf, Xw[0]]
            wz3 = wz.ap.rearrange("p (c n) -> p c n", c=8)
            for c in range(NCH):
                tok = c * 128
                sb_cur = Sbfs[c % 2]
                pO = PS[c % 2]
                mm(pO.ap[:, 0:256], CTt.ap[:, tok:tok + 128], sb_cur.ap, True, True, [CTt, sb_cur], [pO])
                if c < NCH - 1:
                    tt("vector", r3(Sst.ap), r3(Sst.ap), hb(DEC3[:, c, g * 4:g * 4 + 4]), ALU.mult, [Sst, DEC], [Sst])
                    tt("vector", Sst.ap, Sst.ap, Sloc.ap[:, c * 256:(c + 1) * 256], ALU.add, [Sst, Sloc], [Sst])
                    cp("scalar", Sbfs[(c + 1) % 2].ap, Sst.ap, [Sst], [Sbfs[(c + 1) % 2]])
                t1c = t1s[c % 2]
                tt("vector", r3(t1c.ap), r3(pO.ap[:, 0:256]), hb(EY3[:, c, g * 4:g * 4 + 4]), ALU.mult, [pO, EY], [t1c])
                tt("vector", t1c.ap, t1c.ap, ypart.ap[:, c * 256:(c + 1) * 256], ALU.add, [t1c, ypart], [t1c])
                pZ = PS[2 + c % 2]
                for kc in range(8):
                    mm(pZ.ap[:, 0:256], hT3[:, kc, tok:tok + 128], wz3[:, kc, :], kc == 0, kc == 7, [hT, wz], [pZ])
                act(szt.ap, pZ.ap[:, 0:256], AF.Silu, [pZ], [szt])
                tt("vector", t2.ap, t1c.ap, szt.ap, ALU.mult, [t1c, szt], [t2])
                act(junk.ap[:, 0:256], t2.ap, AF.Square, [t2], [junk, sst], accum_out=sst.ap[:, c:c + 1])
                cp("vector", ypart.ap[:, c * 256:(c + 1) * 256], t2.ap, [t2], [ypart])
            act(rstg.ap, sst.ap, AF.Ln, [sst, epst], [rstg], scale=1.0 / 256, bias=epst.ap)
            act(rstg.ap, rstg.ap, AF.Exp, [rstg], [rstg], scale=-0.5)
            for c in range(NCH):
                stt("vector", ot.ap, ypart.ap[:, c * 256:(c + 1) * 256], rstg.ap[:, c:c + 1], snw.ap[:, g * 256:(g + 1) * 256], ALU.mult, ALU.mult,
                    [ypart, rstg, snw], [ot])
                ptile = PT[c % 2]
                for jj in range(2):
                    tr(ptile.bap[:, jj * 128:(jj + 1) * 128], ot.ap[:, jj * 128:(jj + 1) * 128], identb.ap, [ot, identb], [ptile], signal=(jj == 1))
                og = ostg[c % 2]
                cp("vector", og.ap.rearrange("p (j t) -> p j t", j=2), ptile.bap[:, 0:256].rearrange("p (j t) -> p j t", j=2), [ptile], [og])
                r0 = 1024 + g * 256
                cx.dma(mixT[r0:r0 + 256, c * 128:(c + 1) * 128].rearrange("(j p) t -> p j t", p=128), og.ap.rearrange("p (j t) -> p j t", j=2),
                       reads=[og], sem_tile=og)

        cx.mark('P2 end')
        cx.barrier()
        cx.top = p1_keep
        KPAD = 1024
        KW = KPAD + 3200
        wqkv = [cx.alloc(f"wqkv{i}", 8 * 384, BF16) for i in range(2)]
        QT = cx.alloc("QT", MIXW, BF16)
        QM = [cx.alloc(f"QM{i}", MIXW, BF16) for i in range(2)]
        KT = cx.alloc("KT", KW, BF16)
        VTt = cx.alloc("VT", KW, BF16)
        qsw = cx.alloc("qsw", 3200, BF16)
        rtmp = cx.alloc("rtmp", 3200, BF16)
        cosf = cx.alloc("cosf", 3200)
        cost = cx.alloc("cos", 3200, BF16)
        sint = cx.alloc("sin", 3200, BF16)
        amf = cx.alloc("amf", 256)
        amask = cx.alloc("amask", 256, BF16)
        hmask = cx.alloc("hmask", 2)
        NVT = 80
        Vaug = cx.alloc("Vaug", NVT * 130, BF16)
        accs = [cx.alloc(f"acc{i}", MIXW) for i in range(2)]
        denb = cx.alloc("denb", MIXW)
        otmp = cx.alloc("otmp", MIXW, BF16)
        Pt = [cx.alloc(f"P{i}", 512, BF16) for i in range(6)]
        cx.dma(cosf.ap, d_cos, writes=[cosf])
        cp("vector", cost.ap, cosf.ap, [cosf], [cost])
        cx.dma(cosf.ap, d_sin, writes=[cosf])
        cp("vector", sint.ap, cosf.ap, [cosf], [sint])
        cx.dma(amf.ap, d_amask[:, 0:256], writes=[amf])
        cx.dma(hmask.ap, d_hmask, writes=[hmask])
        cp("vector", amask.ap, amf.ap, [amf], [amask])
        KT2 = Tile(cosf.ap[:, 0:(KW + 1) // 2].bitcast(BF16)[:, 0:KW], "KT2")
        KT2.w, KT2.r = cosf.w, dict(cosf.r)
        KTs = [KT, KT2]
        for t_ in (KT, KT2, VTt, qsw, otmp, QT):
            memset("vector", t_, t_.ap, 0.0)
        memset("vector", Vaug, Vaug.ap, 1.0)
        for a_ in accs:
            memset("vector", a_, a_.ap, 1.0)
        Vaug4 = Vaug.ap.rearrange("p (k h d) -> p k h d", h=2, d=65)
        Ma = amask.ap[:, 0:128]
        Mb = amask.ap[:, 128:256]

        def qgroups(n):
            k_ = -(-n // 512)
            base_, rem_ = n // k_, n % k_
            gs, s_ = [], 0
            for i_ in range(k_):
                w_ = base_ + (1 if i_ < rem_ else 0)
                gs.append((s_, w_))
                s_ += w_
            return gs

        def rope_dma(T_, off, n):
            for hh in range(2):
                b0 = hh * 64
                cx.dma(qsw.ap[b0:b0 + 8, 0:n], T_.ap[b0 + 8:b0 + 16, off:off + n], reads=[T_], writes=[qsw], sem_tile=qsw)
                cx.dma(qsw.ap[b0 + 8:b0 + 16, 0:n], T_.ap[b0:b0 + 8, off:off + n], reads=[T_], writes=[qsw], sem_tile=qsw)

        def rope_dve(T_, off, n):
            tt("vector", rtmp.ap[0:80, 0:n], qsw.ap[0:80, 0:n], sint.ap[0:80, 0:n], ALU.mult, [qsw, sint], [rtmp])
            tt("vector", T_.ap[0:80, off:off + n], T_.ap[0:80, off:off + n], cost.ap[0:80, 0:n], ALU.mult, [T_, cost], [T_])
            tt("vector", T_.ap[0:80, off:off + n], T_.ap[0:80, off:off + n], rtmp.ap[0:80, 0:n], ALU.add, [T_, rtmp], [T_])

        vt_index = {}
        vt_list = []

        def vt(dil, tstart_tok):
            key = (dil, tstart_tok)
            if key not in vt_index:
                vt_index[key] = len(vt_list)
                vt_list.append(key)
            return vt_index[key]
        sgroups = []
        for dil in (1, 4, 16):
            if dil == 1:
                for j0 in range(0, 16, 4):
                    qt_ = []
                    for j in range(j0, j0 + 4):
                        qt_.append((128 * j, 128, vt(1, 128 * j - 64), vt(1, 128 * j + 64), j == 0))
                    sgroups.append(dict(d=1, q=qt_, kind=("c", j0 * 128)))
            elif dil == 4:
                for r in range(4):
                    qt_ = []
                    for j in range(4):
                        qt_.append((r + 4 * 128 * j, 128, vt(4, r + 4 * (128 * j - 64)), vt(4, r + 4 * (128 * j + 64)), j == 0))
                    sgroups.append(dict(d=4, q=qt_, kind=("s4", r)))
            else:
                for r0 in range(0, 16, 4):
                    qt_ = []
                    for r in range(r0, r0 + 4):
                        qt_.append((r, 128, vt(16, r - 16 * 64), vt(16, r + 16 * 64), True))
                    sgroups.append(dict(d=16, q=qt_, kind=("s16", r0)))
            sgroups.append(dict(d=dil, q=[(2048, 1, vt(dil, 2048 - 64 * dil), vt(dil, 2048 - 63 * dil), False)], kind=("h", dil)))
        assert len(vt_list) <= NVT, len(vt_list)

        pi = 0
        sgi = 0
        norm_pend = []
        def load_wqkv(hp_):
            wq_ = wqkv[hp_ % 2]
            wq3_ = wq_.ap.rearrange("p (c n) -> p c n", c=8)
            for qi_ in range(3):
                c0 = qi_ * 1024 + hp_ * 128
                wload(wq3_[:, :, qi_ * 128:(qi_ + 1) * 128], w_in[:, c0:c0 + 128], 8, wq_)
        load_wqkv(0)
        ppi = [0]

        def prep_closures(hp_):
            cl = []
            wq_ = wqkv[hp_ % 2]
            wq3_ = wq_.ap.rearrange("p (c n) -> p c n", c=8)
            KTn = KTs[hp_ % 2]

            def grp(which, dst, off, s0, w_):
                def f():
                    pp = PS[4 + ppi[0] % 2]
                    ppi[0] += 1
                    for kc in range(8):
                        mm(pp.ap[:, 0:w_], wq3_[:, kc, which * 128:(which + 1) * 128], hT3[:, kc, s0:s0 + w_], kc == 0, kc == 7, [wq_, hT], [pp])
                    cp("scalar" if (ppi[0] % 2) else "vector", dst.ap[:, off + s0:off + s0 + w_], pp.ap[:, 0:w_], [pp], [dst])
                return f
            for (s0, w_) in qgroups(NKV):
                cl.append(grp(1, KTn, KPAD, s0, w_))
            cl.append(lambda: rope_dma(KTn, KPAD, NKV))
            vg = [grp(2, VTt, KPAD, s0, w_) for (s0, w_) in qgroups(NKV)]
            cl.extend(vg[:3])
            cl.append(lambda: rope_dve(KTn, KPAD, NKV))
            cl.extend(vg[3:])
            for (s0, w_) in qgroups(NQ):
                cl.append(grp(0, QT, 0, s0, w_))
            cl.append(lambda: rope_dma(QT, 0, NQ))
            if hp_ < 7:
                cl.append(lambda: load_wqkv(hp_ + 1))
            return cl

        def post_prep(hp_):
            rope_dve(QT, 0, NQ)
            for hh in range(2):
                act(QM[hh].ap[:, 0:NQ], QT.ap[:, 0:NQ], AF.Copy, [QT, hmask], [QM[hh]], scale=hmask.ap[:, hh:hh + 1])
            for v0 in range(0, len(vt_list), 8):
                ptile = PT[(v0 // 8) % 2]
                nv = min(8, len(vt_list) - v0)
                for vi in range(nv):
                    dil, ts_ = vt_list[v0 + vi]
                    c_ = KPAD + ts_
                    tr(ptile.bap[:, vi * 128:(vi + 1) * 128], VTt.ap[:, c_:c_ + 128 * dil:dil], identb.ap, [VTt, identb], [ptile], signal=(vi == nv - 1))
                cp("vector", Vaug4[:, v0:v0 + nv, :, 0:64], ptile.bap[:, 0:nv * 128].rearrange("p (k h d) -> p k h d", h=2, d=64), [ptile], [Vaug])

        for f_ in prep_closures(0):
            f_()
        post_prep(0)
        for hp in range(8):
            KT = KTs[hp % 2]
            nxt = prep_closures(hp + 1) if hp < 7 else []
            if hp == 7:
                wo = Tile(cx.arena[:, persist_top:persist_top + 8192].bitcast(BF16)[:, 0:16 * 1024], "wo")
                wo.w, wo.r = hT.w, dict(hT.r)
                wo3 = wo.ap.rearrange("p (c n) -> p c n", c=16)
                wload(wo3, w_out, 16, wo)
            while norm_pend:
                norm_pend.pop(0)()
            for hh in range(2):
                acc = accs[hh]
                pend = []
                for sg in sgroups:
                    dil = sg["d"]
                    qts = sg["q"]
                    nqt = len(qts)
                    nq = qts[0][1]
                    bA, bB = PS[2 * (sgi % 2)], PS[2 * (sgi % 2) + 1]
                    PA, PB = Pt[2 * (sgi % 2)], Pt[2 * (sgi % 2) + 1]
                    pO = PS[6 + sgi % 2]
                    sgi += 1
                    for qi, (qs, nq_, va, vb, pad) in enumerate(qts):
                        ka = KPAD + vt_list[va][1]
                        kb = KPAD + vt_list[vb][1]
                        rhs = QM[hh].ap[:, qs:qs + nq_ * dil:dil] if nq_ > 1 else QM[hh].ap[:, qs:qs + 1]
                        mm(bA.ap[:, qi * 128:qi * 128 + nq_], KT.ap[:, ka:ka + 128 * dil:dil], rhs, True, True, [KT, QM[hh]], [bA])
                        mm(bB.ap[:, qi * 128:qi * 128 + nq_], KT.ap[:, kb:kb + 128 * dil:dil], rhs, True, True, [KT, QM[hh]], [bB])
                    A3 = bA.ap.rearrange("p (j q) -> p j q", q=128)[:, 0:nqt, 0:nq]
                    B3 = bB.ap.rearrange("p (j q) -> p j q", q=128)[:, 0:nqt, 0:nq]
                    PA3 = PA.ap.rearrange("p (j q) -> p j q", q=128)[:, 0:nqt, 0:nq]
                    PB3 = PB.ap.rearrange("p (j q) -> p j q", q=128)[:, 0:nqt, 0:nq]
                    act(PA3, A3, AF.Exp, [bA], [PA], scale=0.125)
                    act(PB3, B3, AF.Exp, [bB], [PB], scale=0.125)
                    if nq > 1:
                        tt("vector", PA3, PA3, Ma.unsqueeze(1).to_broadcast([128, nqt, 128]), ALU.mult, [PA, amask], [PA])
                        tt("vector", PB3, PB3, Mb.unsqueeze(1).to_broadcast([128, nqt, 128]), ALU.mult, [PB, amask], [PB])
                        for qi, q_ in enumerate(qts):
                            if q_[4]:
                                memset("vector", PA, PA.ap[0:64, qi * 128:(qi + 1) * 128], 0.0)
                    else:
                        tt("vector", PB3, PB3, amask.ap[:, 127:128].unsqueeze(1), ALU.mult, [PB, amask], [PB])

                    def pv(qts=qts, PA=PA, PB=PB, pO=pO, hh=hh, sg=sg, nq=nq, nqt=nqt, acc=acc):
                        for qi, (qs, nq_, va, vb, pad) in enumerate(qts):
                            op("tensor", lambda e, qi=qi, va=va, nq_=nq_: e.matmul(out=pO.ap[0:65, qi * 128:qi * 128 + nq_], lhsT=Vaug4[:, va, hh, :],
                                                                               rhs=PA.ap[:, qi * 128:qi * 128 + nq_], start=True, stop=False),
                               [Vaug, PA], [pO], signal=False)
                            op("tensor", lambda e, qi=qi, vb=vb, nq_=nq_: e.matmul(out=pO.ap[0:65, qi * 128:qi * 128 + nq_], lhsT=Vaug4[:, vb, hh, :],
                                                                               rhs=PB.ap[:, qi * 128:qi * 128 + nq_], start=False, stop=True),
                               [Vaug, PB], [pO], signal=True)
                        kind, arg = sg["kind"]
                        if kind == "c":
                            cp("scalar", acc.ap[0:65, arg:arg + 512], pO.ap[0:65, 0:512], [pO], [acc])
                        elif kind == "s4":
                            av = acc.ap[0:65, arg:arg + 2048:4]
                            tt("vector", av, av, pO.ap[0:65, 0:512], ALU.add, [acc, pO], [acc])
                        elif kind == "s16":
                            av = acc.ap[0:65, 0:2048].rearrange("p (t r) -> p r t", r=16)[:, arg:arg + 4, :]
                            tt("vector", av, av, pO.ap[0:65, 0:512].rearrange("p (r t) -> p r t", r=4), ALU.add, [acc, pO], [acc])
                        else:
                            if arg == 1:
                                cp("vector", acc.ap[0:65, 2048:2049], pO.ap[0:65, 0:1], [pO], [acc])
                            else:
                                tt("vector", acc.ap[0:65, 2048:2049], acc.ap[0:65, 2048:2049], pO.ap[0:65, 0:1], ALU.add, [acc, pO], [acc])
                    pend.append(pv)
                    if len(pend) > 1:
                        pend.pop(0)()
                    if nxt:
                        nxt.pop(0)()
                while pend:
                    pend.pop(0)()
                while norm_pend:
                    norm_pend.pop(0)()
                drow = hp * 2 + hh
                dtile = Tile(None, f"dscr{drow}")
                cx.dma(dscr[drow:drow + 1, :], acc.ap[64:65, :], reads=[acc], writes=[dtile], sem_tile=acc)
                cx.dma(denb.ap[0:64, :], dscr[drow:drow + 1, :].partition_broadcast(64), reads=[dtile], writes=[denb])

                def norm(acc=acc, hp=hp, hh=hh):
                    act(denb.ap[0:64, :], denb.ap[0:64, :], AF.Ln, [denb], [denb])
                    act(denb.ap[0:64, :], denb.ap[0:64, :], AF.Exp, [denb], [denb], scale=-1.0)
                    tt("vector", otmp.ap[0:64, 0:NQ], acc.ap[0:64, 0:NQ], denb.ap[0:64, 0:NQ], ALU.mult, [acc, denb], [otmp])
                    r0 = hp * 128 + hh * 64
                    cx.dma(mixT[r0:r0 + 64, :], otmp.ap[0:64, :], reads=[otmp], sem_tile=otmp)
                norm_pend.append(norm)
            while nxt:
                nxt.pop(0)()
            if hp < 7:
                post_prep(hp + 1)

        while norm_pend:
            norm_pend.pop(0)()
        cx.mark('P3 end')
        cx.barrier()
        cx.top = persist_top
        wo_res = cx.alloc("wo_res", 8192)
        x1 = cx.alloc("x1", 16 * 1024)
        x1T = [Tile(x1.ap[:, i_ * 1024:(i_ + 1) * 1024], f"x1t{i_}") for i_ in range(16)]
        h2T = cx.alloc("h2T", 8 * MIXW, BF16)
        h2T3 = h2T.ap.rearrange("p (c t) -> p c t", c=8)
        nw2 = cx.alloc("nw2", 8)
        p4_keep = cx.top
        x1b = cx.alloc("x1b", 1024)
        mT = cx.alloc("mT", 16 * 1152, BF16)
        xt2 = [cx.alloc(f"xtb{i}", 1024) for i in range(4)]
        xs2 = [cx.alloc(f"xsb{i}", 1024, BF16) for i in range(2)]
        ss2 = [cx.alloc(f"ssb{i}", 1) for i in range(2)]
        rs2 = [cx.alloc(f"rsb{i}", 1) for i in range(2)]
        cx.dma(nw2.ap, d_nw2, writes=[nw2])
        mT3 = mT.ap.rearrange("p (c t) -> p c t", c=16)
        p4fifo = []
        for i in range(17):
            b = i % 2
            if i in (0, 9):
                t0_, nt_ = (0, 1152) if i == 0 else (1152, 1024)
                for kc in range(16):
                    cx.dma(mT3[:, kc, 0:nt_], mixT[kc * 128:(kc + 1) * 128, t0_:t0_ + nt_], writes=[mT], sem_tile=mT)
            lt = i * 128 - (0 if i < 9 else 1152)
            if i == 0:
                for i2 in range(3):
                    cx.dma(xt2[i2].ap, x[i2 * 128:(i2 + 1) * 128, :], writes=[xt2[i2]])
            if i + 3 < 17:
                cx.dma(xt2[(i + 3) % 4].ap, x[(i + 3) * 128:(i + 4) * 128, :], writes=[xt2[(i + 3) % 4]])
            xtc = xt2[i % 4]
            for nh in range(2):
                pp = PS[(2 * i + nh) % 4]
                for kc in range(16):
                    mm(pp.ap, mT3[:, kc, lt:lt + 128], wo3[:, kc, nh * 512:(nh + 1) * 512], kc == 0, kc == 15, [mT, wo], [pp])
                x1t, x1ap = (x1T[i], x1T[i].ap) if i < 16 else (x1b, x1b.ap)
                tt("vector", x1ap[:, nh * 512:(nh + 1) * 512], pp.ap, xtc.ap[:, nh * 512:(nh + 1) * 512], ALU.add, [pp, xtc], [x1t])
            def normtr(x1t=x1t, x1ap=x1ap, i=i, b=b):
                load_norm_transpose(x1t, x1ap, 128, nw2, h2T3, i * 128, xs2[b], rs2[b], ss2[b], PT[b])
                h2T.w = ("vector", cx.sem["vector"], cx.cnt["vector"])
            p4fifo.append(normtr)
            while len(p4fifo) > 1:
                p4fifo.pop(0)()
        while p4fifo:
            p4fifo.pop(0)()

        cx.mark('P4 end')
        cx.barrier()
        cx.top = p4_keep
        aT_top = cx.top
        GS = 8
        aT = cx.alloc("aT", GS * 2048, BF16)
        _save_top = cx.top
        cx.top = persist_top
        wdg = cx.alloc("wdg", GS * 1024, BF16)
        wstage = cx.alloc("wstage", 8 * 256)
        wup = [cx.alloc(f"wup{i}", 8 * 256, BF16) for i in range(2)]
        assert cx.top <= persist_top + 8192, cx.top
        cx.top = _save_top
        rawf = [cx.alloc(f"rawf{i}", 2052, BF16) for i in range(3)]
        gts = [cx.alloc(f"gt{i}", 2048, BF16) for i in range(2)]
        fcw = cx.alloc("fcw", 132)
        fcb = cx.alloc("fcb", 44)
        diagf = [cx.alloc(f"diagf{i}", 6 * 128, BF16) for i in range(3)]
        ssf = cx.alloc("ssf", 16)
        rsf = cx.alloc("rsf", 16)
        nwf = cx.alloc("nwf", 1024)
        cx.dma(nwf.ap, d_nwf, writes=[nwf])
        cx.dma(fcw.ap, d_fcw, writes=[fcw])
        cx.dma(fcb.ap, d_fcb, writes=[fcb])
        for r_ in rawf:
            memset("vector", r_, r_.ap[:, 0:1], 0.0)
        wdg3 = wdg.ap.rearrange("p (c n) -> p c n", c=GS)
        aT3 = aT.ap.rearrange("p (c t) -> p c t", c=GS)
        ws3 = wstage.ap.rearrange("p (c n) -> p c n", c=8)
        pi = 0
        ri = 0
        ci = 0
        fifo = []
        tgroups = [(i_ * 410 - max(0, i_ - 4), 410 if i_ < 4 else 409) for i_ in range(5)]
        assert tgroups[-1][0] + tgroups[-1][1] == 2049 and all(tgroups[i_][0] + tgroups[i_][1] == tgroups[i_ + 1][0] for i_ in range(4))
        groups = [list(range(g0, min(22, g0 + GS))) for g0 in range(0, 22, GS)]
        for grp in groups:
            for il, i in enumerate(grp):
                wu = wup[i % 2]
                wu3 = wu.ap.rearrange("p (c n) -> p c n", c=8)
                dgf = diagf[i % 3]
                gt = gts[i % 2]
                for part in range(2):
                    c0 = part * 2816 + i * 128
                    cx.dma(ws3[:, :, part * 128:(part + 1) * 128], w_up[:, c0:c0 + 128].rearrange("(c p) n -> p c n", p=128), writes=[wstage])
                    ch = part * 22 + i
                    for k in range(3):
                        ts("vector", dgf.ap[:, (part * 3 + k) * 128:(part * 3 + k + 1) * 128], identf.ap, fcw.ap[:, ch * 3 + k:ch * 3 + k + 1], None,
                           ALU.mult, None, [identf, fcw], [dgf])
                cp("vector", wu.ap, wstage.ap, [wstage], [wu])
                cx.dma(wdg3[:, il, :], w_down[i * 128:(i + 1) * 128, :], writes=[wdg], q="gpsimd")
                for part in range(2):
                    raw = rawf[ri % 3]
                    ri += 1
                    for (s0, w_) in tgroups:
                        pp = PS[pi % 4]
                        pi += 1
                        for kc in range(8):
                            mm(pp.ap[:, 0:w_], wu3[:, kc, part * 128:(part + 1) * 128], h2T3[:, kc, s0:s0 + w_], kc == 0, kc == 7, [wu, h2T], [pp])
                        cp("vector" if (pi % 2) else "scalar", raw.ap[:, 1 + s0:1 + s0 + w_], pp.ap[:, 0:w_], [pp], [raw])
                    ch = part * 22 + i

                    def conv(part=part, raw=raw, dgf=dgf, ch=ch, gt=gt, il=il):
                        global_ci = [0]
                        for tg in range(4):
                            pc = PS[4 + (tg % 2)]
                            for k in range(3):
                                mm(pc.ap, dgf.ap[:, (part * 3 + k) * 128:(part * 3 + k + 1) * 128], raw.ap[:, tg * 512 + k:tg * 512 + k + 512], k == 0, k == 2, [dgf, raw], [pc])
                            if part == 0:
                                act(gt.ap[:, tg * 512:(tg + 1) * 512], pc.ap, AF.Silu, [pc, fcb], [gt], bias=fcb.ap[:, ch:ch + 1])
                            else:
                                stt("vector", aT3[:, il, tg * 512:(tg + 1) * 512], pc.ap, fcb.ap[:, ch:ch + 1], gt.ap[:, tg * 512:(tg + 1) * 512], ALU.add, ALU.mult,
                                    [pc, fcb, gt], [aT])
                    fifo.append(conv)
                    while len(fifo) > 1:
                        fifo.pop(0)()
            while fifo:
                fifo.pop(0)()
            last = (grp is groups[-1])
            for t in range(16):
                for nh in range(2):
                    pp = PS[6 + nh]
                    for il in range(len(grp)):
                        mm(pp.ap, aT3[:, il, t * 128:(t + 1) * 128], wdg3[:, il, nh * 512:(nh + 1) * 512], il == 0, il == len(grp) - 1, [aT, wdg], [pp])
                    tt("vector", x1T[t].ap[:, nh * 512:(nh + 1) * 512], pp.ap, x1T[t].ap[:, nh * 512:(nh + 1) * 512],
                       ALU.add, [pp, x1T[t]], [x1T[t]])
                if last:
                    act(junk.ap, x1T[t].ap, AF.Square, [x1T[t]], [junk, ssf], accum_out=ssf.ap[:, t:t + 1])

        cx.mark('P5 end')
        cx.barrier()
        cx.top = aT_top
        ost = [cx.alloc(f"ost{i}", 1024) for i in range(2)]
        act(rsf.ap, ssf.ap, AF.Ln, [ssf, epst], [rsf], scale=1.0 / 1024, bias=epst.ap)
        act(rsf.ap, rsf.ap, AF.Exp, [rsf], [rsf], scale=-0.5)
        evs = []
        for ti in range(16):
            o_ = ost[ti % 2]
            stt("vector", o_.ap, x1T[ti].ap, rsf.ap[:, ti:ti + 1], nwf.ap, ALU.mult, ALU.mult, [x1T[ti], rsf, nwf], [o_])
            evs.append(cx.dma(out[ti * 128:(ti + 1) * 128, :], o_.ap, reads=[o_], sem_tile=o_))
        for ev in evs[-2:]:
            cx.wait_event("sync", ev)
        if debug:
            for k_, ev in cx.dma_last.items():
                cx.wait_event("sync", ev)
        cx.mark('end')
        cx.emit()
        nc._marks = cx.marks
        nc._abs = cx.abs
    return nc


def _consts(flip):
    c = {}
    c["ident"] = np.eye(128, dtype=np.float32)
    s = np.arange(128)[:, None]
    l = np.arange(128)[None, :]
    c["U"] = (s <= l).astype(np.float32)
    c["UsN"] = -(s < l).astype(np.float32)
    c["ones"] = np.ones((128, 128), np.float32)
    sel = np.zeros((4, 512), np.float32)
    for j in range(4):
        sel[j, j * 128:(j + 1) * 128] = 1.0
    c["sel4"] = sel
    mb = np.zeros((128, 256), np.float32)
    mb[:, 0:128] = np.where(s <= l, 0.0, -30000.0)
    mb[:, 128:256] = np.where(s >= l, 0.0, -30000.0)
    c["maskb"] = mb
    am = np.zeros((128, 17 * 128), np.float32)
    am[:, 0:128] = (s >= l).astype(np.float32)
    am[:, 128:256] = (s <= l).astype(np.float32)
    c["amask"] = am
    half = 8
    inv = np.power(np.float32(500000.0), -np.arange(half, dtype=np.float32) * 2.0 / 16).astype(np.float32)
    pos = np.arange(3200, dtype=np.float32)
    ang = pos[None, :] * inv[:, None]
    cs, sn = np.cos(ang).astype(np.float32), np.sin(ang).astype(np.float32)
    if flip:
        sn = -sn
    cosT = np.ones((128, 3200), np.float32)
    sinT = np.zeros((128, 3200), np.float32)
    for b0 in (0, 64):
        cosT[b0:b0 + 8] = cs
        cosT[b0 + 8:b0 + 16] = cs
        sinT[b0:b0 + 8] = -sn
        sinT[b0 + 8:b0 + 16] = sn
    c["ropecos"] = cosT
    c["ropesin"] = sinT
    hm = np.zeros((128, 2), np.float32)
    hm[0:64, 0] = 1.0
    hm[64:128, 1] = 1.0
    c["hmask"] = hm
    return c


def _chunk_layout(v, n):
    v = np.asarray(v, np.float32)
    k = v.shape[1] if v.ndim == 2 else 1
    return np.ascontiguousarray(v.reshape(n, 128, k).transpose(1, 0, 2).reshape(128, n * k))


def _prep(inputs):
    f = lambda a: np.ascontiguousarray(np.asarray(a, np.float32))
    x = f(inputs["x"])
    maps = []
    w_in = f(inputs["w_in"][0])
    w_in_flip = w_in.copy()
    w_in_flip[:, 6144:6160] = w_in[:, 6160:6176]
    w_in_flip[:, 6160:6176] = w_in[:, 6144:6160]
    base = {
        "w_out": f(inputs["w_out"][0]), "w_up": f(inputs["w_up"][0]), "w_down": f(inputs["w_down"][0]),
        "nw1": _chunk_layout(inputs["norm1_w"][0], 8), "nw2": _chunk_layout(inputs["norm2_w"][0], 8),
        "nwf_bc": np.ascontiguousarray(np.broadcast_to(f(inputs["final_norm_w"])[None, :], (128, 1024))),
        "cb": _chunk_layout(inputs["ssm_conv_b"][0], 16), "fcb": _chunk_layout(inputs["ffn_conv_b"][0], 44),
        "dskip_bc": np.ascontiguousarray(np.broadcast_to(f(inputs["d_skip"][0])[None, :], (128, 16))),
        "ssm_nw_bc": np.ascontiguousarray(np.broadcast_to(f(inputs["ssm_norm_w"][0])[None, :], (128, 1024))),
    }
    for flip in (0, 1):
        d = dict(base)
        d.update(_consts(flip))
        cw = f(inputs["ssm_conv_w"][0])
        fcw = f(inputs["ffn_conv_w"][0])
        af, ab = f(inputs["a_log_f"][0]), f(inputs["a_log_b"][0])
        bf_, bb = f(inputs["dt_bias_f"][0]), f(inputs["dt_bias_b"][0])
        if flip:
            cw, fcw = cw[:, ::-1], fcw[:, ::-1]
            af, ab, bf_, bb = ab, af, bb, bf_
        d["cw"] = _chunk_layout(cw, 16)
        d["fcw"] = _chunk_layout(fcw, 44)
        d["alog_bc"] = np.ascontiguousarray(np.broadcast_to(np.concatenate([af, ab])[None, :], (128, 32)))
        d["dtb_bc"] = np.ascontiguousarray(np.broadcast_to(np.concatenate([bf_, bb])[None, :], (128, 32)))
        d["w_in"] = w_in_flip if flip else w_in
        maps.append(d)
    in_maps = []
    for c in range(8):
        b, flip = c // 2, c % 2
        d = dict(maps[flip])
        d["x"] = np.ascontiguousarray(x[b, ::-1]) if flip else np.ascontiguousarray(x[b])
        in_maps.append(d)
    return in_maps


def kernel(**inputs):
    in_maps = _prep(inputs)
    nc = build_nc()
    res = run_bass_kernel_spmd(nc, in_maps, core_ids=list(range(8)))
    outf = np.zeros((4, 4096, 1024), np.float32)
    for c in range(8):
        o = np.asarray(res.results[c]["out"], np.float32)
        b, flip = c // 2, c % 2
        if flip:
            outf[b, 2048:] = o[::-1]
        else:
            outf[b, :2048] = o
    return outf
```
